# Optimizing a Trainium2 kernel written in Bass

```python
import jax, jax.numpy as jnp
from jax import lax
import numpy as np

D_MODEL = 4096
BATCH = 4
SEQ = 4096
DEPTH = 1

CHUNK = 64
N_BRANCH = 2
MIX_WIDTH = D_MODEL
BRANCH_WIDTH = MIX_WIDTH // 2
GMLP_WIDTH = BRANCH_WIDTH
GMLP_BLOCK = 128
GMLP_GROUP_DIM = 128
GMLP_GROUPS = GMLP_WIDTH // GMLP_GROUP_DIM
HG_DK = 128
HG_DV = 128
HG_WIDTH = BRANCH_WIDTH
HG_HEADS = HG_WIDTH // HG_DK
PROJ_COLS = 3 * GMLP_WIDTH + 5 * HG_WIDTH + N_BRANCH * D_MODEL
EPS = 1e-6

kernel_name = "gmlp_hgrn2_gated_hybrid_block"


def rmsnorm(x, w):
    x32 = x.astype(jnp.float32)
    y = x32 * lax.rsqrt(jnp.mean(x32 * x32, axis=-1, keepdims=True) + EPS)
    return y.astype(x.dtype) * w


def layernorm(x, w, b):
    x32 = x.astype(jnp.float32)
    mu = jnp.mean(x32, axis=-1, keepdims=True)
    xc = x32 - mu
    y = xc * lax.rsqrt(jnp.mean(xc * xc, axis=-1, keepdims=True) + EPS)
    return y.astype(x.dtype) * w + b


def gmlp_spatial_gate(u, v, ln_w, ln_b, w_s, b_s):
    bsz, seq, width = v.shape
    nb = seq // GMLP_BLOCK
    v = layernorm(v, ln_w, ln_b).reshape(bsz, nb, GMLP_BLOCK, GMLP_GROUPS, GMLP_GROUP_DIM)
    chunk_id = jnp.arange(GMLP_BLOCK) // CHUNK
    mask = chunk_id[None, :] <= chunk_id[:, None]
    w_m = jnp.where(mask[None], w_s, jnp.zeros((), w_s.dtype))
    mixed = jnp.einsum('gij,bnjgc->bnigc', w_m, v) + b_s.T[:, :, None]
    return u * mixed.reshape(bsz, seq, width)


def hgrn2_chunkwise(q, f_logit, inp, lb):
    bsz, seq, _ = q.shape
    n = seq // CHUNK
    f = lb + (1.0 - lb) * jax.nn.sigmoid(f_logit.astype(jnp.float32))
    g = jnp.log(f).reshape(bsz, n, CHUNK, HG_HEADS, HG_DK)
    k = (1.0 - f).reshape(bsz, n, CHUNK, HG_HEADS, HG_DK)
    qf = jax.nn.silu(q.astype(jnp.float32)).reshape(bsz, n, CHUNK, HG_HEADS, HG_DK)
    v = inp.astype(jnp.float32).reshape(bsz, n, CHUNK, HG_HEADS, HG_DV)
    c = jnp.cumsum(g, axis=2)
    c_mid = c[:, :, CHUNK // 2 - 1:CHUNK // 2]
    q_in = qf * jnp.exp(c - c_mid)
    k_in = k * jnp.exp(c_mid - c)
    scores = jnp.einsum('bnihd,bnjhd->bnhij', q_in, k_in)
    causal = jnp.tril(jnp.ones((CHUNK, CHUNK), dtype=bool))
    scores = jnp.where(causal, scores, 0.0)
    o_intra = jnp.einsum('bnhij,bnjhv->bnihv', scores, v)
    c_last = c[:, :, -1:]
    q_dec = qf * jnp.exp(c)
    k_dec = k * jnp.exp(c_last - c)
    chunk_decay = jnp.exp(c_last[:, :, 0])

    def step(state, xs):
        qd, kd, vv, dec = xs
        o = jnp.einsum('bihd,bhdv->bihv', qd, state)
        state = dec[..., None] * state + jnp.einsum('bjhd,bjhv->bhdv', kd, vv)
        return state, o

    s0 = jnp.zeros((bsz, HG_HEADS, HG_DK, HG_DV), jnp.float32)
    xs = (jnp.moveaxis(q_dec, 1, 0), jnp.moveaxis(k_dec, 1, 0),
          jnp.moveaxis(v, 1, 0), jnp.moveaxis(chunk_decay, 1, 0))
    _, o_inter = lax.scan(step, s0, xs)
    o = o_intra + jnp.moveaxis(o_inter, 0, 1)
    return o.reshape(bsz, seq, HG_HEADS, HG_DV)


def setup_inputs(seed: int = 0) -> dict:
    key = jax.random.key(seed)
    ks = jax.random.split(key, 14)
    f32 = jnp.float32
    x = jax.random.normal(ks[0], (BATCH, SEQ, D_MODEL), f32)
    norm_w = 1.0 + 0.02 * jax.random.normal(ks[1], (DEPTH, D_MODEL), f32)
    w_in = jax.random.normal(ks[2], (DEPTH, D_MODEL, PROJ_COLS), f32) * D_MODEL ** -0.5
    gmlp_ln_w = 1.0 + 0.02 * jax.random.normal(ks[3], (DEPTH, GMLP_WIDTH), f32)
    gmlp_ln_b = 0.02 * jax.random.normal(ks[4], (DEPTH, GMLP_WIDTH), f32)
    gmlp_w_s = jax.random.normal(ks[5], (DEPTH, GMLP_GROUPS, GMLP_BLOCK, GMLP_BLOCK), f32) * GMLP_BLOCK ** -0.5
    gmlp_b_s = 1.0 + 0.02 * jax.random.normal(ks[6], (DEPTH, GMLP_GROUPS, GMLP_BLOCK), f32)
    hgrn_lb_logits = 0.1 * jax.random.normal(ks[7], (DEPTH + 1, HG_WIDTH), f32)
    hgrn_norm_w = 1.0 + 0.02 * jax.random.normal(ks[8], (DEPTH, HG_WIDTH), f32)
    w_branch = jax.random.normal(ks[9], (DEPTH, N_BRANCH, BRANCH_WIDTH, D_MODEL), f32) * BRANCH_WIDTH ** -0.5
    w_out = jax.random.normal(ks[10], (DEPTH, D_MODEL, D_MODEL), f32) * D_MODEL ** -0.5
    final_norm_w = 1.0 + 0.02 * jax.random.normal(ks[11], (D_MODEL,), f32)
    return {"x": x, "norm_w": norm_w, "w_in": w_in, "gmlp_ln_w": gmlp_ln_w, "gmlp_ln_b": gmlp_ln_b,
            "gmlp_w_s": gmlp_w_s, "gmlp_b_s": gmlp_b_s, "hgrn_lb_logits": hgrn_lb_logits,
            "hgrn_norm_w": hgrn_norm_w, "w_branch": w_branch, "w_out": w_out,
            "final_norm_w": final_norm_w}


def reference(x, norm_w, w_in, gmlp_ln_w, gmlp_ln_b, gmlp_w_s, gmlp_b_s, hgrn_lb_logits,
              hgrn_norm_w, w_branch, w_out, final_norm_w):
    bsz, seq, _ = x.shape
    lb_all = jnp.cumsum(jax.nn.softmax(hgrn_lb_logits.astype(jnp.float32), axis=0), axis=0)
    split_pts = np.cumsum([GMLP_WIDTH] * 3 + [HG_WIDTH] * 5)
    for l in range(DEPTH):
        h = rmsnorm(x, norm_w[l])
        proj = h @ w_in[l]
        u, v, z_a, q, f_logit, inp, og, z_b, gates = jnp.split(proj, split_pts, axis=-1)
        h_a = gmlp_spatial_gate(jax.nn.gelu(u), jax.nn.gelu(v), gmlp_ln_w[l], gmlp_ln_b[l],
                                gmlp_w_s[l], gmlp_b_s[l]) * jax.nn.silu(z_a)
        o = hgrn2_chunkwise(q, f_logit, inp, lb_all[l])
        o32 = o * lax.rsqrt(jnp.mean(o * o, axis=-1, keepdims=True) + EPS)
        o = o32.reshape(bsz, seq, HG_WIDTH).astype(x.dtype) * hgrn_norm_w[l]
        h_b = o * jax.nn.sigmoid(og) * jax.nn.silu(z_b)
        g_a, g_b = jnp.split(jax.nn.sigmoid(gates), N_BRANCH, axis=-1)
        merged = g_a * (h_a @ w_branch[l, 0]) + g_b * (h_b @ w_branch[l, 1])
        x = x + merged @ w_out[l]
    return rmsnorm(x, final_norm_w)
```

```python
import numpy as np
import concourse.bass as bass
import concourse.mybir as mybir
from concourse.bass_utils import run_bass_kernel_spmd

F32 = mybir.dt.float32
BF16 = mybir.dt.bfloat16
AF = mybir.ActivationFunctionType
ALU = mybir.AluOpType

D = 4096
T = 512
GW = 256
EPS = 1e-6
COL = dict(u=0, v=2048, za=4096, q=6144, f=8192, inp=10240, og=12288, zb=14336, ga=16384, gb=20480)
NPG = 128
REUSE_BF16 = True
N_PRECONV = 40


class Buf:
    __slots__ = ("name", "last_w", "readers")

    def __init__(self, name):
        self.name = name
        self.last_w = None
        self.readers = {}


class Prog:
    ENGS = ("pe", "act", "dve", "pool", "sp")

    def __init__(self, nc):
        self.nc = nc
        self.ops = {e: [] for e in self.ENGS}
        self.sems = {}
        self.cnt = {}
        self.seen = {e: {} for e in self.ENGS}
        self.nops = {e: 0 for e in self.ENGS}
        for e in self.ENGS:
            self._mksem("c_" + e)

    def _mksem(self, key):
        self.sems[key] = self.nc.alloc_semaphore(name=key)
        self.cnt[key] = 0

    def _collect(self, eng, reads, writes):
        waits = {}

        def need(tok, raw):
            if tok is None:
                return
            key, val, teng, tidx = tok
            if teng == eng and key == "c_" + eng:
                if eng == "pe" or not raw:
                    return
                if tidx < self.nops[eng] - 2:
                    return
            if val > waits.get(key, 0):
                waits[key] = val

        for b in reads:
            need(b.last_w, True)
        for b in writes:
            need(b.last_w, False)
            for r in b.readers.values():
                need(r, False)
        out = []
        for key, val in waits.items():
            if self.seen[eng].get(key, 0) >= val:
                continue
            self.seen[eng][key] = val
            out.append((key, val))
        return out

    @staticmethod
    def _flat(xs):
        out = []
        for b in xs:
            if isinstance(b, (list, tuple)):
                out.extend(Prog._flat(b))
            else:
                out.append(b)
        return out

    def op(self, eng, fn, reads=(), writes=(), dma=None):
        reads = self._flat(reads)
        writes = self._flat(writes)
        waits = self._collect(eng, reads, writes)
        if dma is None:
            key = "c_" + eng
            inc = 1
        else:
            key = dma
            if key not in self.sems:
                self._mksem(key)
            inc = 16
        self.cnt[key] += inc
        val = self.cnt[key]
        tok = (key, val, eng, self.nops[eng])
        self.nops[eng] += 1
        sems = self.sems
        sem = sems[key]

        def run(e, waits=waits, fn=fn, sem=sem, inc=inc):
            for k, v in waits:
                e.wait_ge(sems[k], v)
            ins = fn(e)
            ins.then_inc(sem, inc)

        self.ops[eng].append(run)
        for b in writes:
            b.last_w = tok
            b.readers = {}
        for b in reads:
            b.readers[key] = tok
        return tok

    def wait_all(self, eng, toks):
        sems = self.sems
        ws = {}
        for key, val, _, _ in toks:
            ws[key] = max(ws.get(key, 0), val)

        def run(e, ws=ws):
            for k, v in ws.items():
                e.wait_ge(sems[k], v)

        self.ops[eng].append(run)

    def emit(self):
        nc = self.nc
        with nc.Block() as block:
            @block.tensor
            def _(e):
                for f in self.ops["pe"]:
                    f(e)

            @block.scalar
            def _(e):
                for f in self.ops["act"]:
                    f(e)

            @block.vector
            def _(e):
                for f in self.ops["dve"]:
                    f(e)

            @block.gpsimd
            def _(e):
                for f in self.ops["pool"]:
                    f(e)

            @block.sync
            def _(e):
                for f in self.ops["sp"]:
                    f(e)


def build(n_own, n_warm, dbg=False):
    nc = bass.Bass("TRN2", target_bir_lowering=False)

    def dram(name, shape, kind="ExternalInput"):
        return nc.dram_tensor(name, shape, F32, kind=kind).ap()

    x = dram("x", [n_own * T, D])
    xp = dram("xp", [max(n_warm, 1) * T, D])
    w_in = dram("w_in", [D, 24576])
    w_br = dram("w_br", [D, D])
    w_out = dram("w_out", [D, D])
    out = dram("out", [n_own * T, D], kind="ExternalOutput")
    c_ident = dram("c_ident", [128, 128])
    c_cmask = dram("c_cmask", [128, 128])
    c_gmask = dram("c_gmask", [128, 128])
    c_smask = dram("c_smask", [128, 512])
    c_lohi = dram("c_lohi", [128, 2])
    p_normw = dram("p_normw", [128, 32])
    p_lw = dram("p_lw", [128, 16])
    p_hgw = dram("p_hgw", [128, 16])
    p_l0 = dram("p_l0", [128, 16])
    p_l1 = dram("p_l1", [128, 16])
    p_fw = dram("p_fw", [128, D])
    p_ws = dram("p_ws", [128, 16 * 128])
    p_bs = dram("p_bs", [128, 16])
    p_lnb = dram("p_lnb", [128, 2048])

    P = Prog(nc)
    sb = nc.alloc_sbuf_tensor
    if dbg:
        d_hT = nc.dram_tensor("d_hT", [128, 32 * T], BF16, kind="ExternalOutput").ap()
        d_hA = nc.dram_tensor("d_hA", [128, 16 * T], BF16, kind="ExternalOutput").ap()
        d_hB = nc.dram_tensor("d_hB", [128, 16 * T], BF16, kind="ExternalOutput").ap()
        d_mg = nc.dram_tensor("d_mg", [128, 32 * T], BF16, kind="ExternalOutput").ap()

    arena = sb("arena", [128, NPG * 256], F32)
    pgb = [Buf("pg%d" % i) for i in range(NPG)]

    def A32(p0, n):
        return arena[:, p0 * 256:(p0 + n) * 256]

    def A16(p0, n):
        return arena[:, p0 * 256:(p0 + n) * 256].bitcast(BF16)

    def PB(p0, n):
        return pgb[p0:p0 + n]

    ident_f = sb("ident_f", [128, 128], F32); b_ident_f = Buf("ident_f")
    ident_b = sb("ident_b", [128, 128], BF16); b_ident_b = Buf("ident_b")
    cmask = sb("cmask", [128, 128], F32); b_cmask = Buf("cmask")
    smask = sb("smask", [128, 512], F32); b_smask = Buf("smask")
    lohi = sb("lohi", [128, 2], F32); b_lohi = Buf("lohi")
    normw = sb("normw", [128, 32], F32); b_normw = Buf("normw")
    lw = sb("lw", [128, 16], F32); b_lw = Buf("lw")
    hgw = sb("hgw", [128, 16], F32); b_hgw = Buf("hgw")
    l0 = sb("l0", [128, 16], F32); b_l0 = Buf("l0")
    l1 = sb("l1", [128, 16], F32); b_l1 = Buf("l1")
    lbt = sb("lbt", [128, 16], F32); b_lbt = Buf("lbt")
    oml = sb("oml", [128, 16], F32); b_oml = Buf("oml")
    ones_f = sb("ones_f", [128, 128], BF16); b_ones = Buf("ones")
    WmT = sb("WmT", [128, 16, 128], BF16); b_WmT = Buf("WmT")
    C2 = sb("C2", [128, 16, 128], F32); b_C2 = Buf("C2")
    S_f = sb("S_f", [128, 16, 128], F32); b_Sf = [Buf("Sf%d" % h) for h in range(16)]
    S_b = sb("S_b", [128, 16, 128], BF16); b_Sb = [Buf("Sb%d" % h) for h in range(16)]
    NSLOT = 3
    Wt = [sb("W%d" % i, [128, 32, GW], BF16) for i in range(NSLOT)]
    b_W = [Buf("W%d" % i) for i in range(NSLOT)]
    stat = sb("stat", [128, 64], F32)
    b_stat = [Buf("stat%d" % i) for i in range(64)]
    bnst = sb("bnst", [128, 4, 6], F32); b_bnst = Buf("bnst")
    mv = sb("mv", [128, 2], F32); b_mv = Buf("mv")

    NBANK = 8
    psum = [nc.alloc_psum_tensor("ps%d" % i, [128, 512], F32) for i in range(NBANK)]
    b_psh = [[Buf("ps%d_%d" % (i, hh)) for hh in range(2)] for i in range(NBANK)]
    b_ps = b_psh
    bank_ctr = [0]

    def bank():
        i = bank_ctr[0] % NBANK
        bank_ctr[0] += 1
        return i

    stat_ctr = [0]

    def stat_slot():
        i = stat_ctr[0] % 64
        stat_ctr[0] += 1
        return i

    wctr = [0]
    NGRP = 128
    wbf = nc.dram_tensor("wbf", [NGRP, 128, 32 * GW], BF16, kind="Internal").ap()
    b_wbf = [Buf("wbf%d" % i) for i in range(NGRP)]
    converted = set()

    def load_w(src_ap, gid, nk=32):
        s = wctr[0] % NSLOT
        wctr[0] += 1
        if gid in converted:
            src = wbf[gid, :, 0:nk * GW].rearrange("p (k c) -> p k c", c=GW)
            P.op("sp", lambda e, s=s, src=src, nk=nk: e.dma_start(out=Wt[s][:, 0:nk, :], in_=src),
                 reads=[b_wbf[gid]], writes=[b_W[s]], dma="w%d" % s)
        else:
            src = src_ap.rearrange("(k p) c -> p k c", p=128)
            P.op("pool", lambda e, s=s, src=src, nk=nk: e.dma_start(out=Wt[s][:, 0:nk, :], in_=src),
                 writes=[b_W[s]], dma="w%d" % s)
            if REUSE_BF16:
                dst = wbf[gid, :, 0:nk * GW].rearrange("p (k c) -> p k c", c=GW)
                P.op("sp", lambda e, s=s, dst=dst, nk=nk: e.dma_start(out=dst, in_=Wt[s][:, 0:nk, :]),
                     reads=[b_W[s]], writes=[b_wbf[gid]], dma="wb%d" % s)
                converted.add(gid)
        return s

    pcctr = [0]
    b_pc = [Buf("pc%d" % i) for i in range(4)]

    def preconvert(src_ap, gid, nk=32):
        if gid in converted:
            return
        i = pcctr[0] % 4
        pcctr[0] += 1
        src = src_ap.rearrange("(k p) c -> p k c", p=128)
        dst = wbf[gid, :, 0:nk * GW].rearrange("p (k c) -> p k c", c=GW)
        P.op("pool", lambda e, src=src, dst=dst: e.dma_start(out=dst, in_=src), writes=[b_wbf[gid], b_pc[i]], dma="pc%d" % i)
        converted.add(gid)

    def preconvert_plan(n):
        order = []
        for nm in ("u", "v", "za"):
            for g in range(8):
                order.append((nm, g))
        for hp in range(8):
            for nm in ("q", "og", "zb"):
                order.append((nm, hp))
        for nm, g in order[:n]:
            preconvert(w_in[:, COL[nm] + g * GW: COL[nm] + (g + 1) * GW], COL[nm] // GW + g)

    def ld(dst, src, bufs, q="sp", key="ld_c"):
        P.op(q, lambda e: e.dma_start(out=dst, in_=src), writes=bufs, dma=key)

    ld(ident_f[:], c_ident, [b_ident_f], key="cc1")
    ld(cmask[:], c_cmask, [b_cmask], key="cc2")
    ld(lohi[:], c_lohi, [b_lohi], key="cc_lohi")
    ld(smask[:], c_smask, [b_smask], key="cc3")
    ld(normw[:], p_normw, [b_normw], key="cc4")
    ld(lw[:], p_lw, [b_lw], key="cc5")
    ld(hgw[:], p_hgw, [b_hgw], key="cc6")
    ld(l0[:], p_l0, [b_l0], key="cc7")
    ld(l1[:], p_l1, [b_l1], key="cc8")
    SC = 64
    ws_raw = A32(SC, 8)
    bs_col = A32(SC + 8, 1)[:, 0:16]
    lnb_bc = A32(SC + 16, 8)
    gmask = A32(SC + 24, 1)[:, 0:128]
    wm_tmp = A32(SC + 25, 1)[:, 0:128]
    wmT_f = A32(SC + 26, 1)[:, 0:128]
    cs_col = A32(SC + 27, 1)[:, 0:1]
    c2t = A32(SC + 28, 1)[:, 0:128]
    ld(ws_raw, p_ws, PB(SC, 8), key="cc9")
    ld(bs_col, p_bs, PB(SC + 8, 1), key="cc10")
    ld(lnb_bc, p_lnb, PB(SC + 16, 8), key="cc11")
    ld(gmask, c_gmask, PB(SC + 24, 1), key="cc12")

    P.op("dve", lambda e: e.memset(ones_f[:], 1.0), writes=[b_ones])
    P.op("dve", lambda e: e.memset(S_f[:], 0.0), writes=b_Sf)
    P.op("dve", lambda e: e.memset(S_b[:], 0.0), writes=b_Sb)
    P.op("dve", lambda e: e.tensor_copy(out=ident_b[:], in_=ident_f[:]), reads=[b_ident_f], writes=[b_ident_b])
    P.op("dve", lambda e: e.tensor_tensor(out=lbt[:], in0=l1[:], in1=l0[:], op=ALU.subtract), reads=[b_l0, b_l1], writes=[b_lbt])
    P.op("act", lambda e: e.activation(out=oml[:], in_=lbt[:], func=AF.Sigmoid), reads=[b_lbt], writes=[b_oml])

    for g in range(16):
        P.op("dve", lambda e, g=g: e.tensor_tensor(out=wm_tmp, in0=ws_raw[:, g * 128:(g + 1) * 128], in1=gmask, op=ALU.mult),
             reads=PB(SC, 8) + PB(SC + 24, 1), writes=PB(SC + 25, 1))
        bk = bank()
        P.op("pe", lambda e, bk=bk: e.transpose(out=psum[bk][:, 0:128], in_=wm_tmp, identity=ident_f[:]),
             reads=PB(SC + 25, 1) + [b_ident_f], writes=[b_ps[bk]])
        P.op("act", lambda e, bk=bk, g=g: e.activation(out=WmT[:, g, :], in_=psum[bk][:, 0:128], func=AF.Copy),
             reads=[b_ps[bk]], writes=[b_WmT])
        P.op("dve", lambda e: e.tensor_reduce(out=cs_col, in_=wm_tmp, axis=mybir.AxisListType.X, op=ALU.add),
             reads=PB(SC + 25, 1), writes=PB(SC + 27, 1))
        P.op("dve", lambda e, g=g: e.tensor_scalar(out=c2t, in0=lnb_bc[:, g * 128:(g + 1) * 128], scalar1=cs_col,
                                                    scalar2=bs_col[:, g:g + 1], op0=ALU.mult, op1=ALU.add),
             reads=PB(SC + 16, 8) + PB(SC + 27, 1) + PB(SC + 8, 1), writes=PB(SC + 28, 1))
        bk3 = bank()
        P.op("pe", lambda e, bk3=bk3: e.transpose(out=psum[bk3][:, 0:128], in_=c2t, identity=ident_f[:]),
             reads=PB(SC + 28, 1) + [b_ident_f], writes=[b_ps[bk3]])
        P.op("act", lambda e, bk3=bk3, g=g: e.activation(out=C2[:, g, :], in_=psum[bk3][:, 0:128], func=AF.Copy),
             reads=[b_ps[bk3]], writes=[b_C2])

    HT0, HA0, HB0 = 0, 32, 48
    hT = A16(HT0, 32).rearrange("p (k t) -> p k t", t=T)
    hA = A16(HA0, 16).rearrange("p (k t) -> p k t", t=T)
    hB = A16(HB0, 16).rearrange("p (k t) -> p k t", t=T)

    out_toks = []

    def phase0(src, r0):
        for blk in range(4):
            xp0 = SC + 16 * (blk % 2)
            xb = A32(xp0, 16)
            junk = A16(SC + 32, 8)
            P.op("sp", lambda e, xb=xb, r=r0 + blk * 128: e.dma_start(out=xb, in_=src[r:r + 128, :]),
                 writes=PB(xp0, 16), dma="x%d" % (blk % 2))
            s0 = stat_slot(); s1 = stat_slot(); s2 = stat_slot()
            P.op("act", lambda e, xb=xb, s0=s0: e.activation(out=junk, in_=xb, func=AF.Square, accum_out=stat[:, s0:s0 + 1]),
                 reads=PB(xp0, 16), writes=PB(SC + 32, 8) + [b_stat[s0]])
            P.op("dve", lambda e, s0=s0, s1=s1: e.tensor_scalar(out=stat[:, s1:s1 + 1], in0=stat[:, s0:s0 + 1], scalar1=1.0 / D,
                                                                 scalar2=EPS, op0=ALU.mult, op1=ALU.add),
                 reads=[b_stat[s0]], writes=[b_stat[s1]])
            P.op("act", lambda e, s1=s1, s2=s2: e.activation(out=stat[:, s2:s2 + 1], in_=stat[:, s1:s1 + 1], func=AF.Ln),
                 reads=[b_stat[s1]], writes=[b_stat[s2]])
            P.op("act", lambda e, s1=s1, s2=s2: e.activation(out=stat[:, s1:s1 + 1], in_=stat[:, s2:s2 + 1], func=AF.Exp, scale=-0.5),
                 reads=[b_stat[s2]], writes=[b_stat[s1]])
            P.op("dve", lambda e, xb=xb, s1=s1: e.tensor_scalar(out=xb, in0=xb, scalar1=stat[:, s1:s1 + 1], scalar2=None, op0=ALU.mult),
                 reads=PB(xp0, 16) + [b_stat[s1]], writes=PB(xp0, 16))
            for kb in range(8):
                bk = bank()

                def tr(e, bk=bk, kb=kb, xb=xb):
                    ins = None
                    for j in range(4):
                        kc = kb * 4 + j
                        ins = e.transpose(out=psum[bk][:, j * 128:(j + 1) * 128], in_=xb[:, kc * 128:(kc + 1) * 128],
                                          identity=ident_f[:])
                    return ins
                P.op("pe", tr, reads=PB(xp0, 16) + [b_ident_f], writes=[b_ps[bk]])
                for j in range(4):
                    kc = kb * 4 + j
                    if False:
                        pass
                    else:
                        P.op("dve", lambda e, bk=bk, j=j, kc=kc, blk=blk: e.tensor_scalar(
                            out=hT[:, kc, blk * 128:(blk + 1) * 128], in0=psum[bk][:, j * 128:(j + 1) * 128],
                            scalar1=normw[:, kc:kc + 1], scalar2=None, op0=ALU.mult),
                            reads=[b_ps[bk], b_normw], writes=PB(HT0 + kc, 1))

    def proj_fm(s, nk, rhs_of, rhs_bufs, koff=0):
        bks = []
        for cb in range(2):
            bk = bank()
            bks.append(bk)

            def mm(e, bk=bk, cb=cb):
                ins = None
                for kc in range(nk):
                    ins = e.matmul(psum[bk][:, :], lhsT=Wt[s][:, koff + kc, cb * 128:(cb + 1) * 128], rhs=rhs_of(kc),
                                   start=(kc == 0), stop=(kc == nk - 1))
                return ins
            P.op("pe", mm, reads=[b_W[s]] + rhs_bufs, writes=[b_ps[bk]])
        return bks

    def proj_tm(s, lhs_of, lhs_bufs):
        res = []
        bk = None
        for blk in range(4):
            bk = bank()
            c0 = 0

            def mm(e, bk=bk, blk=blk, c0=c0):
                ins = None
                for kc in range(32):
                    ins = e.matmul(psum[bk][:, c0:c0 + GW], lhsT=lhs_of(kc, blk), rhs=Wt[s][:, kc, :],
                                   start=(kc == 0), stop=(kc == 31))
                return ins
            P.op("pe", mm, reads=[b_W[s]] + lhs_bufs, writes=[b_ps[bk]])
            res.append((bk, c0))
        return res

    hT_bufs = PB(HT0, 32)

    def hT_rhs(kc):
        return hT[:, kc, :]

    def hT_lhs(kc, blk):
        return hT[:, kc, blk * 128:(blk + 1) * 128]

    def phase1():
        for ug in range(8):
            s = load_w(w_in[:, COL["u"] + ug * GW: COL["u"] + (ug + 1) * GW], COL["u"] // GW + ug)
            bks = proj_fm(s, 32, hT_rhs, hT_bufs)
            for cb, bk in enumerate(bks):
                g = ug * 2 + cb
                P.op("act", lambda e, bk=bk, g=g: e.activation(out=hA[:, g, :], in_=psum[bk][:, :], func=AF.Gelu_apprx_tanh),
                     reads=[b_ps[bk]], writes=PB(HA0 + g, 1))
        gv = A16(SC, 16).rearrange("p (b c) -> p b c", c=2048)
        for vg in range(8):
            s = load_w(w_in[:, COL["v"] + vg * GW: COL["v"] + (vg + 1) * GW], COL["v"] // GW + vg)
            res = proj_tm(s, hT_lhs, hT_bufs)
            for blk, (bk, c0) in enumerate(res):
                P.op("act", lambda e, bk=bk, c0=c0, blk=blk, vg=vg: e.activation(
                    out=gv[:, blk, vg * GW:(vg + 1) * GW], in_=psum[bk][:, c0:c0 + GW], func=AF.Gelu_apprx_tanh),
                    reads=[b_ps[bk]], writes=PB(SC + blk * 4 + vg // 2, 1))
        for blk in range(4):
            for c in range(4):
                P.op("dve", lambda e, blk=blk, c=c: e.bn_stats(out=bnst[:, c, :], in_=gv[:, blk, c * 512:(c + 1) * 512]),
                     reads=PB(SC + blk * 4 + c, 1), writes=[b_bnst])
            P.op("dve", lambda e: e.bn_aggr(out=mv[:], in_=bnst[:].rearrange("p a b -> p (a b)")), reads=[b_bnst], writes=[b_mv])
            s1 = stat_slot(); s2 = stat_slot(); s3 = stat_slot()
            P.op("dve", lambda e, s1=s1: e.tensor_scalar(out=stat[:, s1:s1 + 1], in0=mv[:, 1:2], scalar1=EPS, scalar2=None, op0=ALU.add),
                 reads=[b_mv], writes=[b_stat[s1]])
            P.op("act", lambda e, s1=s1, s2=s2: e.activation(out=stat[:, s2:s2 + 1], in_=stat[:, s1:s1 + 1], func=AF.Ln),
                 reads=[b_stat[s1]], writes=[b_stat[s2]])
            P.op("act", lambda e, s1=s1, s2=s2: e.activation(out=stat[:, s1:s1 + 1], in_=stat[:, s2:s2 + 1], func=AF.Exp, scale=-0.5),
                 reads=[b_stat[s2]], writes=[b_stat[s1]])
            P.op("dve", lambda e, s1=s1, s3=s3: e.tensor_scalar(out=stat[:, s3:s3 + 1], in0=mv[:, 0:1], scalar1=stat[:, s1:s1 + 1],
                                                                 scalar2=-1.0, op0=ALU.mult, op1=ALU.mult),
                 reads=[b_mv, b_stat[s1]], writes=[b_stat[s3]])
            P.op("dve", lambda e, blk=blk, s1=s1, s3=s3: e.tensor_scalar(
                out=gv[:, blk, :], in0=gv[:, blk, :], scalar1=stat[:, s1:s1 + 1], scalar2=stat[:, s3:s3 + 1],
                op0=ALU.mult, op1=ALU.add),
                reads=PB(SC + blk * 4, 4) + [b_stat[s1], b_stat[s3]], writes=PB(SC + blk * 4, 4))
        for g in range(16):
            bk = bank()

            def mm(e, bk=bk, g=g):
                ins = None
                for blk in range(4):
                    ins = e.matmul(psum[bk][:, blk * 128:(blk + 1) * 128], lhsT=gv[:, blk, g * 128:(g + 1) * 128],
                                   rhs=WmT[:, g, :], start=True, stop=True)
                return ins
            P.op("pe", mm, reads=PB(SC, 16) + [b_WmT], writes=[b_ps[bk]])
            tp = SC + 16 + 2 * (g % 2)
            t1 = A32(tp, 2)
            P.op("dve", lambda e, bk=bk, g=g, t1=t1: e.scalar_tensor_tensor(
                out=t1.rearrange("p (b i) -> p b i", i=128), in0=psum[bk][:, :].rearrange("p (b i) -> p b i", i=128),
                scalar=lw[:, g:g + 1], in1=C2[:, g:g + 1, :].broadcast_to([128, 4, 128]), op0=ALU.mult, op1=ALU.add),
                reads=[b_ps[bk], b_lw, b_C2], writes=PB(tp, 2))
            P.op("dve", lambda e, g=g, t1=t1: e.tensor_tensor(out=hA[:, g, :], in0=hA[:, g, :], in1=t1, op=ALU.mult),
                 reads=PB(tp, 2) + PB(HA0 + g, 1), writes=PB(HA0 + g, 1))
        for zg in range(8):
            s = load_w(w_in[:, COL["za"] + zg * GW: COL["za"] + (zg + 1) * GW], COL["za"] // GW + zg)
            bks = proj_fm(s, 32, hT_rhs, hT_bufs)
            for cb, bk in enumerate(bks):
                g = zg * 2 + cb
                tp = SC + 20 + 2 * (g % 2)
                t1 = A32(tp, 2)
                P.op("act", lambda e, bk=bk, t1=t1: e.activation(out=t1, in_=psum[bk][:, :], func=AF.Silu),
                     reads=[b_ps[bk]], writes=PB(tp, 2))
                P.op("dve", lambda e, g=g, t1=t1: e.tensor_tensor(out=hA[:, g, :], in0=hA[:, g, :], in1=t1, op=ALU.mult),
                     reads=PB(tp, 2) + PB(HA0 + g, 1), writes=PB(HA0 + g, 1))

    INP0 = SC
    inpT = A16(INP0, 16).rearrange("p (b c) -> p b c", c=2048)

    def head_temps(hl):
        base = SC + 16 + hl * 24
        d = {}
        names = [("tk", 2), ("tq", 2), ("tf", 2), ("tc", 2), ("kp", 1), ("kd", 2), ("kdT", 2), ("qd", 1), ("sT", 1),
                 ("ver", 2), ("sog", 1), ("szb", 1), ("oc", 2), ("osq", 2)]
        p = base
        for n, k in names:
            d[n] = (p, k)
            p += k
        assert p <= base + 24
        d["Sv"] = (base, 4)
        return d

    class HC:
        def __init__(self, hl):
            self.tm = head_temps(hl)

        def v32(self, n):
            return A32(*self.tm[n])

        def v16(self, n):
            return A16(*self.tm[n])

        def b(self, n):
            return PB(*self.tm[n])

    def evac_f(hl, bk):
        c = HC(hl)
        P.op("act", lambda e, bk=bk, tk=c.v32("tk"): e.activation(out=tk, in_=psum[bk][:, :], func=AF.Sigmoid, scale=-1.0),
             reads=[b_ps[bk]], writes=c.b("tk"))

    def fchain(h, hl):
        c = HC(hl)
        tk, tf, tc = c.v32("tk"), c.v32("tf"), c.v32("tc")
        kp, kd = c.v16("kp"), c.v32("kd")
        P.op("dve", lambda e: e.tensor_scalar(out=tk, in0=tk, scalar1=oml[:, h:h + 1], scalar2=None, op0=ALU.mult),
             reads=c.b("tk") + [b_oml], writes=c.b("tk"))
        P.op("dve", lambda e: e.tensor_scalar(out=tf, in0=tk, scalar1=-1.0, scalar2=1.0, op0=ALU.mult, op1=ALU.add),
             reads=c.b("tk"), writes=c.b("tf"))
        P.op("act", lambda e: e.activation(out=tf, in_=tf, func=AF.Ln), reads=c.b("tf"), writes=c.b("tf"))
        P.op("dve", lambda e: e.tensor_tensor_scan(out=tc, data0=smask[:], data1=tf, initial=0.0, op0=ALU.mult, op1=ALU.add),
             reads=c.b("tf") + [b_smask], writes=c.b("tc"))
        P.op("act", lambda e: e.activation(out=tf, in_=tc, func=AF.Exp), reads=c.b("tc"), writes=c.b("tf"))
        P.op("act", lambda e: e.activation(out=tc, in_=tc, func=AF.Exp, scale=-1.0), reads=c.b("tc"), writes=c.b("tc"))
        P.op("dve", lambda e: e.tensor_tensor(out=kp, in0=tk, in1=tc, op=ALU.mult),
             reads=c.b("tk") + c.b("tc"), writes=c.b("kp"))
        tA3 = tf.rearrange("p (n i) -> p n i", i=64)
        P.op("dve", lambda e: e.tensor_tensor(
            out=kd.rearrange("p (n i) -> p n i", i=64), in0=kp.rearrange("p (n i) -> p n i", i=64),
            in1=tA3[:, :, 63:64].broadcast_to([128, 8, 64]), op=ALU.mult),
            reads=c.b("kp") + c.b("tf"), writes=c.b("kd"))

    def kdt_stage(h, hl):
        c = HC(hl)
        kd, kdT = c.v32("kd"), c.v16("kdT")
        bkT = bank()

        def trk(e):
            ins = None
            for blk in range(4):
                ins = e.transpose(out=psum[bkT][:, blk * 128:(blk + 1) * 128], in_=kd[:, blk * 128:(blk + 1) * 128],
                                  identity=ident_f[:])
            return ins
        P.op("pe", trk, reads=c.b("kd") + [b_ident_f], writes=[b_ps[bkT]])
        P.op("act", lambda e: e.activation(out=kdT[:, 0:512], in_=psum[bkT][:, :], func=AF.Identity, scale=lohi[:, 0:1]),
             reads=[b_ps[bkT], b_lohi], writes=c.b("kdT"))
        P.op("dve", lambda e: e.tensor_scalar(out=kdT[:, 512:1024], in0=psum[bkT][:, :], scalar1=lohi[:, 1:2], scalar2=None,
                                              op0=ALU.mult),
             reads=[b_ps[bkT], b_lohi], writes=c.b("kdT"))

    def state_stage(h, hl, own):
        c = HC(hl)
        kdT, tf = c.v16("kdT"), c.v32("tf")
        bkU = [bank(), bank()]

        def umm(e):
            ins = None
            for n in range(8):
                blk, ch = n // 2, n % 2
                ins = e.matmul(psum[bkU[n // 4]][:, (n % 4) * 128:(n % 4 + 1) * 128],
                               lhsT=kdT[:, ch * 512 + blk * 128:ch * 512 + (blk + 1) * 128],
                               rhs=inpT[:, blk, h * 128:(h + 1) * 128], start=True, stop=True)
            return ins
        P.op("pe", umm, reads=c.b("kdT") + PB(INP0, 16), writes=[b_ps[bkU[0]], b_ps[bkU[1]]])
        Sv = A32(*c.tm["Sv"]).rearrange("p (n v) -> p n v", v=128)
        ver = c.v16("ver").rearrange("p (n v) -> p n v", v=128)
        if own:
            P.op("act", lambda e: e.activation(out=ver[:, 0, :], in_=S_b[:, h, :], func=AF.Copy),
                 reads=[b_Sb[h]], writes=c.b("ver"))
        for n in range(8):
            prev = S_f[:, h, :] if n == 0 else Sv[:, n - 1, :]
            P.op("dve", lambda e, n=n, prev=prev: e.scalar_tensor_tensor(
                out=Sv[:, n, :], in0=prev, scalar=tf[:, n * 64 + 63:n * 64 + 64],
                in1=psum[bkU[n // 4]][:, (n % 4) * 128:(n % 4 + 1) * 128], op0=ALU.mult, op1=ALU.add),
                reads=[b_Sf[h], b_ps[bkU[n // 4]]] + c.b("tf") + c.b("Sv"), writes=c.b("Sv"))
        if own:
            P.op("act", lambda e: e.activation(out=ver[:, 1:8, :], in_=Sv[:, 0:7, :], func=AF.Copy),
                 reads=c.b("Sv"), writes=c.b("ver"))
        P.op("act", lambda e: e.activation(out=S_b[:, h, :], in_=Sv[:, 7, :], func=AF.Copy), reads=c.b("Sv"), writes=[b_Sb[h]])
        P.op("dve", lambda e: e.tensor_copy(out=S_f[:, h, :], in_=Sv[:, 7, :]), reads=c.b("Sv"), writes=[b_Sf[h]])

    def score_stage(h, hl):
        c = HC(hl)
        kp, qd, sT = c.v16("kp"), c.v16("qd"), c.v16("sT")
        bkS = bank()

        def smm(e):
            ins = None
            for blk in range(4):
                ins = e.matmul(psum[bkS][:, blk * 128:(blk + 1) * 128], lhsT=kp[:, blk * 128:(blk + 1) * 128],
                               rhs=qd[:, blk * 128:(blk + 1) * 128], start=True, stop=True)
            return ins
        P.op("pe", smm, reads=c.b("kp") + c.b("qd"), writes=[b_ps[bkS]])
        P.op("dve", lambda e: e.tensor_tensor(
            out=sT.rearrange("p (b i) -> p b i", i=128), in0=psum[bkS][:, :].rearrange("p (b i) -> p b i", i=128),
            in1=cmask[:].rearrange("p (o i) -> p o i", o=1).broadcast_to([128, 4, 128]), op=ALU.mult),
            reads=[b_ps[bkS], b_cmask], writes=c.b("sT"))

    def out_stage(h, hl):
        c = HC(hl)
        qd, sT = c.v16("qd"), c.v16("sT")
        ver = c.v16("ver").rearrange("p (n v) -> p n v", v=128)
        bkO = bank()
        bkI = bank()

        def omm(e):
            ins = None
            for blk in range(4):
                e.matmul(psum[bkO][:, blk * 128:(blk + 1) * 128], lhsT=inpT[:, blk, h * 128:(h + 1) * 128],
                         rhs=sT[:, blk * 128:(blk + 1) * 128], start=True, stop=True)
            for n in range(8):
                c0 = n * 64
                ins = e.matmul(psum[bkI][:, c0:c0 + 64], lhsT=ver[:, n, :], rhs=qd[:, c0:c0 + 64], start=True, stop=True)
            return ins
        P.op("pe", omm, reads=c.b("sT") + c.b("qd") + c.b("ver") + PB(INP0, 16), writes=[b_ps[bkO], b_ps[bkI]])
        oc, osq = c.v32("oc"), c.v32("osq")
        osqb = c.v16("osq")[:, 0:512]
        P.op("act", lambda e: e.activation(out=oc, in_=psum[bkI][:, :], func=AF.Copy), reads=[b_ps[bkI]], writes=c.b("oc"))
        P.op("dve", lambda e: e.tensor_tensor(out=oc, in0=psum[bkO][:, :], in1=oc, op=ALU.add),
             reads=[b_ps[bkO]] + c.b("oc"), writes=c.b("oc"))
        P.op("act", lambda e: e.activation(out=osqb, in_=oc, func=AF.Square), reads=c.b("oc"), writes=c.b("osq"))

    def norm_stage(h, hl):
        c = HC(hl)
        oc, osq = c.v32("oc"), c.v32("osq")
        osqb = c.v16("osq")[:, 0:512]
        sog, szb = c.v16("sog"), c.v16("szb")
        bkN = bank()
        P.op("pe", lambda e: e.matmul(psum[bkN][:, :], lhsT=ones_f[:], rhs=osqb, start=True, stop=True),
             reads=c.b("osq") + [b_ones], writes=[b_ps[bkN]])
        P.op("dve", lambda e: e.tensor_scalar(out=osq, in0=psum[bkN][:, :], scalar1=1.0 / 128, scalar2=EPS, op0=ALU.mult, op1=ALU.add),
             reads=[b_ps[bkN]], writes=c.b("osq"))
        P.op("act", lambda e: e.activation(out=osq, in_=osq, func=AF.Ln), reads=c.b("osq"), writes=c.b("osq"))
        P.op("act", lambda e: e.activation(out=osq, in_=osq, func=AF.Exp, scale=-0.5), reads=c.b("osq"), writes=c.b("osq"))
        P.op("dve", lambda e: e.tensor_tensor(out=oc, in0=oc, in1=osq, op=ALU.mult), reads=c.b("oc") + c.b("osq"), writes=c.b("oc"))
        P.op("dve", lambda e: e.tensor_tensor(out=oc, in0=oc, in1=sog, op=ALU.mult), reads=c.b("oc") + c.b("sog"), writes=c.b("oc"))
        P.op("dve", lambda e: e.scalar_tensor_tensor(out=hB[:, h, :], in0=oc, scalar=hgw[:, h:h + 1], in1=szb,
                                                     op0=ALU.mult, op1=ALU.mult),
             reads=c.b("oc") + c.b("szb") + [b_hgw], writes=PB(HB0 + h, 1))

    def wproj(name, hp):
        s = load_w(w_in[:, COL[name] + hp * GW: COL[name] + (hp + 1) * GW], COL[name] // GW + hp)
        return proj_fm(s, 32, hT_rhs, hT_bufs)

    def phase2(own):
        for ig in range(8):
            s = load_w(w_in[:, COL["inp"] + ig * GW: COL["inp"] + (ig + 1) * GW], COL["inp"] // GW + ig)
            res = proj_tm(s, hT_lhs, hT_bufs)
            for blk, (bk, c0) in enumerate(res):
                P.op("act", lambda e, bk=bk, c0=c0, blk=blk, ig=ig: e.activation(
                    out=inpT[:, blk, ig * GW:(ig + 1) * GW], in_=psum[bk][:, c0:c0 + GW], func=AF.Copy),
                    reads=[b_ps[bk]], writes=PB(INP0 + blk * 4 + ig // 2, 1))
        if not own:
            bkF = wproj("f", 0)
            for hl in range(2):
                evac_f(hl, bkF[hl])
            for hl in range(2):
                fchain(hl, hl)
            for hp in range(8):
                if hp + 1 < 8:
                    bkF = wproj("f", hp + 1)
                for hl in range(2):
                    kdt_stage(hp * 2 + hl, hl)
                for hl in range(2):
                    state_stage(hp * 2 + hl, hl, False)
                if hp + 1 < 8:
                    for hl in range(2):
                        evac_f(hl, bkF[hl])
                    for hl in range(2):
                        fchain((hp + 1) * 2 + hl, hl)
            return
        for hp in range(8):
            bkF = wproj("f", hp)
            for hl in range(2):
                evac_f(hl, bkF[hl])
            for hl in range(2):
                fchain(hp * 2 + hl, hl)
            bkQ = wproj("q", hp)
            for hl in range(2):
                c = HC(hl)
                P.op("act", lambda e, bq=bkQ[hl], tq=c.v32("tq"): e.activation(out=tq, in_=psum[bq][:, :], func=AF.Silu),
                     reads=[b_ps[bkQ[hl]]], writes=c.b("tq"))
            for hl in range(2):
                c = HC(hl)
                P.op("dve", lambda e, tq=c.v32("tq"), tf=c.v32("tf"), qd=c.v16("qd"): e.tensor_tensor(out=qd, in0=tq, in1=tf, op=ALU.mult),
                     reads=c.b("tq") + c.b("tf"), writes=c.b("qd"))
            bkO2 = wproj("og", hp)
            for hl in range(2):
                c = HC(hl)
                P.op("act", lambda e, b=bkO2[hl], t=c.v16("sog"): e.activation(out=t, in_=psum[b][:, :], func=AF.Sigmoid),
                     reads=[b_ps[bkO2[hl]]], writes=c.b("sog"))
            bkZ = wproj("zb", hp)
            for hl in range(2):
                c = HC(hl)
                P.op("act", lambda e, b=bkZ[hl], t=c.v16("szb"): e.activation(out=t, in_=psum[b][:, :], func=AF.Silu),
                     reads=[b_ps[bkZ[hl]]], writes=c.b("szb"))
            for hl in range(2):
                kdt_stage(hp * 2 + hl, hl)
            for hl in range(2):
                score_stage(hp * 2 + hl, hl)
            for hl in range(2):
                state_stage(hp * 2 + hl, hl, True)
            for hl in range(2):
                out_stage(hp * 2 + hl, hl)
            for hl in range(2):
                norm_stage(hp * 2 + hl, hl)

    MG0 = SC
    merged = A16(MG0, 32).rearrange("p (k t) -> p k t", t=T)

    def phase3():
        for dg in range(16):
            sA = load_w(w_in[:, COL["ga"] + dg * GW: COL["ga"] + (dg + 1) * GW], COL["ga"] // GW + dg)
            bkGA = proj_fm(sA, 32, hT_rhs, hT_bufs)
            sB = load_w(w_in[:, COL["gb"] + dg * GW: COL["gb"] + (dg + 1) * GW], COL["gb"] // GW + dg)
            bkGB = proj_fm(sB, 32, hT_rhs, hT_bufs)
            sW = load_w(w_br[:, dg * GW:(dg + 1) * GW], 96 + dg)
            bkPA = proj_fm(sW, 16, lambda kc: hA[:, kc, :], PB(HA0, 16), koff=0)
            bkPB = proj_fm(sW, 16, lambda kc: hB[:, kc, :], PB(HB0, 16), koff=16)
            for cb in range(2):
                dmb = dg * 2 + cb
                tp = SC + 32 + 4 * cb
                ta, tb = A32(tp, 2), A32(tp + 2, 2)
                P.op("act", lambda e, b=bkGA[cb], ta=ta: e.activation(out=ta, in_=psum[b][:, :], func=AF.Sigmoid),
                     reads=[b_ps[bkGA[cb]]], writes=PB(tp, 2))
                P.op("act", lambda e, b=bkGB[cb], tb=tb: e.activation(out=tb, in_=psum[b][:, :], func=AF.Sigmoid),
                     reads=[b_ps[bkGB[cb]]], writes=PB(tp + 2, 2))
                P.op("dve", lambda e, b=bkPA[cb], ta=ta: e.tensor_tensor(out=ta, in0=psum[b][:, :], in1=ta, op=ALU.mult),
                     reads=[b_ps[bkPA[cb]]] + PB(tp, 2), writes=PB(tp, 2))
                P.op("dve", lambda e, b=bkPB[cb], tb=tb: e.tensor_tensor(out=tb, in0=psum[b][:, :], in1=tb, op=ALU.mult),
                     reads=[b_ps[bkPB[cb]]] + PB(tp + 2, 2), writes=PB(tp + 2, 2))
                P.op("dve", lambda e, ta=ta, tb=tb, dmb=dmb: e.tensor_tensor(out=merged[:, dmb, :], in0=ta, in1=tb, op=ALU.add),
                     reads=PB(tp, 4), writes=PB(MG0 + dmb, 1))

    Y0 = 0
    yv = A32(Y0, 64).rearrange("p (b c) -> p b c", c=D)
    FW0 = SC + 32
    fwb = A32(FW0, 16)

    def phase4(r0):
        for blk in range(4):
            P.op("sp", lambda e, blk=blk: e.dma_start(out=yv[:, blk, :], in_=x[r0 + blk * 128:r0 + (blk + 1) * 128, :]),
                 writes=PB(Y0 + blk * 16, 16), dma="y%d" % blk)
        P.op("sp", lambda e: e.dma_start(out=fwb, in_=p_fw), writes=PB(FW0, 16), dma="fw")
        for og in range(16):
            s = load_w(w_out[:, og * GW:(og + 1) * GW], 112 + og)
            res = proj_tm(s, lambda kc, blk: merged[:, kc, blk * 128:(blk + 1) * 128], PB(MG0, 32))
            for blk, (bk, c0) in enumerate(res):
                P.op("dve", lambda e, bk=bk, c0=c0, blk=blk, og=og: e.tensor_tensor(
                    out=yv[:, blk, og * GW:(og + 1) * GW], in0=psum[bk][:, c0:c0 + GW], in1=yv[:, blk, og * GW:(og + 1) * GW], op=ALU.add),
                    reads=[b_ps[bk]] + PB(Y0 + blk * 16 + og, 1), writes=PB(Y0 + blk * 16 + og, 1))
        junk = A16(MG0, 8)
        for blk in range(4):
            s0 = stat_slot(); s1 = stat_slot(); s2 = stat_slot()
            P.op("act", lambda e, blk=blk, s0=s0: e.activation(out=junk, in_=yv[:, blk, :], func=AF.Square, accum_out=stat[:, s0:s0 + 1]),
                 reads=PB(Y0 + blk * 16, 16), writes=PB(MG0, 8) + [b_stat[s0]])
            P.op("dve", lambda e, s0=s0, s1=s1: e.tensor_scalar(out=stat[:, s1:s1 + 1], in0=stat[:, s0:s0 + 1], scalar1=1.0 / D,
                                                                 scalar2=EPS, op0=ALU.mult, op1=ALU.add),
                 reads=[b_stat[s0]], writes=[b_stat[s1]])
            P.op("act", lambda e, s1=s1, s2=s2: e.activation(out=stat[:, s2:s2 + 1], in_=stat[:, s1:s1 + 1], func=AF.Ln),
                 reads=[b_stat[s1]], writes=[b_stat[s2]])
            P.op("act", lambda e, s1=s1, s2=s2: e.activation(out=stat[:, s1:s1 + 1], in_=stat[:, s2:s2 + 1], func=AF.Exp, scale=-0.5),
                 reads=[b_stat[s2]], writes=[b_stat[s1]])
            P.op("dve", lambda e, blk=blk, s1=s1: e.scalar_tensor_tensor(
                out=yv[:, blk, :], in0=yv[:, blk, :], scalar=stat[:, s1:s1 + 1], in1=fwb, op0=ALU.mult, op1=ALU.mult),
                reads=PB(Y0 + blk * 16, 16) + PB(FW0, 16) + [b_stat[s1]], writes=PB(Y0 + blk * 16, 16))
            tok = P.op("sp", lambda e, blk=blk: e.dma_start(out=out[r0 + blk * 128:r0 + (blk + 1) * 128, :], in_=yv[:, blk, :]),
                       reads=PB(Y0 + blk * 16, 16), dma="o%d" % blk)
            out_toks.append(tok)

    import os as _os
    stages = _os.environ.get("K_STAGES", "w0,w2,p0,p1,p2,p3,p4").split(",")
    for t in range(n_warm):
        if "w0" in stages:
            phase0(xp, t * T)
        if "w2" in stages:
            phase2(False)
        if t == 0 and n_warm > 1 and REUSE_BF16 and N_PRECONV > 0:
            preconvert_plan(N_PRECONV)
    for t in range(n_own):
        if "p0" in stages:
            phase0(x, t * T)
        if "p1" in stages:
            phase1()
        if "p2" in stages:
            phase2(True)
        if dbg and t == n_own - 1:
            out_toks.append(P.op("sp", lambda e: e.dma_start(out=d_hT, in_=A16(HT0, 32)), reads=PB(HT0, 32), dma="dbg0"))
            out_toks.append(P.op("sp", lambda e: e.dma_start(out=d_hA, in_=A16(HA0, 16)), reads=PB(HA0, 16), dma="dbg1"))
            out_toks.append(P.op("sp", lambda e: e.dma_start(out=d_hB, in_=A16(HB0, 16)), reads=PB(HB0, 16), dma="dbg2"))
        if "p3" in stages:
            phase3()
        if dbg and t == n_own - 1:
            out_toks.append(P.op("sp", lambda e: e.dma_start(out=d_mg, in_=A16(MG0, 32)), reads=PB(MG0, 32), dma="dbg3"))
        if "p4" in stages:
            phase4(t * T)
    P.wait_all("sp", out_toks)
    P.emit()
    return nc


def _consts():
    ident = np.eye(128, dtype=np.float32)
    idx = np.arange(128)
    ch = idx // 64
    cmask = ((ch[:, None] == ch[None, :]) & (idx[:, None] <= idx[None, :])).astype(np.float32)
    gmask = (ch[None, :] <= ch[:, None]).astype(np.float32)
    smask = np.ones((128, 512), np.float32)
    smask[:, ::64] = 0.0
    lohi = np.zeros((128, 2), np.float32)
    lohi[:64, 0] = 1.0
    lohi[64:, 1] = 1.0
    return ident, cmask, gmask, smask, lohi


def _param_maps(norm_w, gmlp_ln_w, gmlp_ln_b, gmlp_w_s, gmlp_b_s, hgrn_lb_logits, hgrn_norm_w, final_norm_w):
    ident, cmask, gmask, smask, lohi = _consts()
    f = np.float32
    return {
        "c_ident": ident, "c_cmask": cmask, "c_gmask": gmask, "c_smask": smask, "c_lohi": lohi,
        "p_normw": np.ascontiguousarray(norm_w[0].reshape(32, 128).T, dtype=f),
        "p_lw": np.ascontiguousarray(gmlp_ln_w[0].reshape(16, 128).T, dtype=f),
        "p_hgw": np.ascontiguousarray(hgrn_norm_w[0].reshape(16, 128).T, dtype=f),
        "p_l0": np.ascontiguousarray(hgrn_lb_logits[0].reshape(16, 128).T, dtype=f),
        "p_l1": np.ascontiguousarray(hgrn_lb_logits[1].reshape(16, 128).T, dtype=f),
        "p_fw": np.ascontiguousarray(np.broadcast_to(final_norm_w[None, :], (128, D)), dtype=f),
        "p_ws": np.ascontiguousarray(np.transpose(gmlp_w_s[0], (1, 0, 2)).reshape(128, 2048), dtype=f),
        "p_bs": np.ascontiguousarray(gmlp_b_s[0].T, dtype=f),
        "p_lnb": np.ascontiguousarray(np.broadcast_to(gmlp_ln_b[0][None, :], (128, 2048)), dtype=f),
    }


_NC_CACHE = {}


def kernel(x, norm_w, w_in, gmlp_ln_w, gmlp_ln_b, gmlp_w_s, gmlp_b_s, hgrn_lb_logits, hgrn_norm_w,
           w_branch, w_out, final_norm_w):
    x = np.asarray(x, dtype=np.float32)
    Bsz, S, _ = x.shape
    half = S // 2
    n_own = half // T
    n_warm = half // T
    key = (n_own, n_warm)
    if key not in _NC_CACHE:
        _NC_CACHE[key] = build(n_own, n_warm)
    nc = _NC_CACHE[key]
    pm = _param_maps(np.asarray(norm_w), np.asarray(gmlp_ln_w), np.asarray(gmlp_ln_b), np.asarray(gmlp_w_s),
                     np.asarray(gmlp_b_s), np.asarray(hgrn_lb_logits), np.asarray(hgrn_norm_w), np.asarray(final_norm_w))
    w_in0 = np.ascontiguousarray(np.asarray(w_in)[0], dtype=np.float32)
    w_br0 = np.ascontiguousarray(np.asarray(w_branch)[0].reshape(2 * 2048, D), dtype=np.float32)
    w_out0 = np.ascontiguousarray(np.asarray(w_out)[0], dtype=np.float32)
    zeros = np.zeros((half, D), np.float32)
    in_maps = []
    for c in range(8):
        b, hf = c // 2, c % 2
        m = dict(pm)
        m["x"] = np.ascontiguousarray(x[b, hf * half:(hf + 1) * half])
        m["xp"] = zeros if hf == 0 else np.ascontiguousarray(x[b, 0:half])
        m["w_in"] = w_in0
        m["w_br"] = w_br0
        m["w_out"] = w_out0
        in_maps.append(m)
    res = run_bass_kernel_spmd(nc, in_maps, core_ids=list(range(8)))
    outp = np.empty((Bsz, S, D), np.float32)
    for c in range(8):
        b, hf = c // 2, c % 2
        outp[b, hf * half:(hf + 1) * half] = res.results[c]["out"]
    return outp
```

```python
import numpy as np
import concourse.bass as bass
import concourse.mybir as mybir
from concourse.bass_utils import run_bass_kernel_spmd

F32 = mybir.dt.float32
BF16 = mybir.dt.bfloat16
AF = mybir.ActivationFunctionType
ALU = mybir.AluOpType

D = 4096
T = 512
GW = 256
EPS = 1e-6
COL = dict(u=0, v=2048, za=4096, q=6144, f=8192, inp=10240, og=12288, zb=14336, ga=16384, gb=20480)
NPG = 128
REUSE_BF16 = True
N_PRECONV = 48


class Buf:
    __slots__ = ("name", "last_w", "readers")

    def __init__(self, name):
        self.name = name
        self.last_w = None
        self.readers = {}


class Prog:
    ENGS = ("pe", "act", "dve", "pool", "sp")

    def __init__(self, nc):
        self.nc = nc
        self.ops = {e: [] for e in self.ENGS}
        self.sems = {}
        self.cnt = {}
        self.seen = {e: {} for e in self.ENGS}
        self.nops = {e: 0 for e in self.ENGS}
        for e in self.ENGS:
            self._mksem("c_" + e)

    def _mksem(self, key):
        self.sems[key] = self.nc.alloc_semaphore(name=key)
        self.cnt[key] = 0

    def _collect(self, eng, reads, writes):
        waits = {}

        def need(tok, raw):
            if tok is None:
                return
            key, val, teng, tidx = tok
            if teng == eng and key == "c_" + eng:
                if eng == "pe" or not raw:
                    return
                if tidx < self.nops[eng] - 2:
                    return
            if val > waits.get(key, 0):
                waits[key] = val

        for b in reads:
            need(b.last_w, True)
        for b in writes:
            need(b.last_w, False)
            for r in b.readers.values():
                need(r, False)
        out = []
        for key, val in waits.items():
            if self.seen[eng].get(key, 0) >= val:
                continue
            self.seen[eng][key] = val
            out.append((key, val))
        return out

    @staticmethod
    def _flat(xs):
        out = []
        for b in xs:
            if isinstance(b, (list, tuple)):
                out.extend(Prog._flat(b))
            else:
                out.append(b)
        return out

    def op(self, eng, fn, reads=(), writes=(), dma=None):
        reads = self._flat(reads)
        writes = self._flat(writes)
        waits = self._collect(eng, reads, writes)
        if dma is None:
            key = "c_" + eng
            inc = 1
        else:
            key = dma
            if key not in self.sems:
                self._mksem(key)
            inc = 16
        self.cnt[key] += inc
        val = self.cnt[key]
        tok = (key, val, eng, self.nops[eng])
        self.nops[eng] += 1
        sems = self.sems
        sem = sems[key]

        def run(e, waits=waits, fn=fn, sem=sem, inc=inc):
            for k, v in waits:
                e.wait_ge(sems[k], v)
            ins = fn(e)
            ins.then_inc(sem, inc)

        self.ops[eng].append(run)
        for b in writes:
            b.last_w = tok
            b.readers = {}
        for b in reads:
            b.readers[key] = tok
        return tok

    def wait_all(self, eng, toks):
        sems = self.sems
        ws = {}
        for key, val, _, _ in toks:
            ws[key] = max(ws.get(key, 0), val)

        def run(e, ws=ws):
            for k, v in ws.items():
                e.wait_ge(sems[k], v)

        self.ops[eng].append(run)

    def emit(self):
        nc = self.nc
        with nc.Block() as block:
            @block.tensor
            def _(e):
                for f in self.ops["pe"]:
                    f(e)

            @block.scalar
            def _(e):
                for f in self.ops["act"]:
                    f(e)

            @block.vector
            def _(e):
                for f in self.ops["dve"]:
                    f(e)

            @block.gpsimd
            def _(e):
                for f in self.ops["pool"]:
                    f(e)

            @block.sync
            def _(e):
                for f in self.ops["sp"]:
                    f(e)


def build(n_own, n_warm, dbg=False):
    nc = bass.Bass("TRN2", target_bir_lowering=False)

    def dram(name, shape, kind="ExternalInput"):
        return nc.dram_tensor(name, shape, F32, kind=kind).ap()

    x = dram("x", [n_own * T, D])
    xp = dram("xp", [max(n_warm, 1) * T, D])
    w_in = dram("w_in", [D, 24576])
    w_br = dram("w_br", [D, D])
    w_out = dram("w_out", [D, D])
    out = dram("out", [n_own * T, D], kind="ExternalOutput")
    c_ident = dram("c_ident", [128, 128])
    c_cmask = dram("c_cmask", [128, 128])
    c_gmask = dram("c_gmask", [128, 128])
    c_smask = dram("c_smask", [128, 512])
    c_lohi = dram("c_lohi", [128, 2])
    p_normw = dram("p_normw", [128, 32])
    p_lw = dram("p_lw", [128, 16])
    p_hgw = dram("p_hgw", [128, 16])
    p_l0 = dram("p_l0", [128, 16])
    p_l1 = dram("p_l1", [128, 16])
    p_fw = dram("p_fw", [128, D])
    p_ws = dram("p_ws", [128, 16 * 128])
    p_bs = dram("p_bs", [128, 16])
    p_lnb = dram("p_lnb", [128, 2048])

    P = Prog(nc)
    sb = nc.alloc_sbuf_tensor
    if dbg:
        d_hT = nc.dram_tensor("d_hT", [128, 32 * T], BF16, kind="ExternalOutput").ap()
        d_hA = nc.dram_tensor("d_hA", [128, 16 * T], BF16, kind="ExternalOutput").ap()
        d_hB = nc.dram_tensor("d_hB", [128, 16 * T], BF16, kind="ExternalOutput").ap()
        d_mg = nc.dram_tensor("d_mg", [128, 32 * T], BF16, kind="ExternalOutput").ap()

    arena = sb("arena", [128, NPG * 256], F32)
    pgb = [Buf("pg%d" % i) for i in range(NPG)]

    def A32(p0, n):
        return arena[:, p0 * 256:(p0 + n) * 256]

    def A16(p0, n):
        return arena[:, p0 * 256:(p0 + n) * 256].bitcast(BF16)

    def PB(p0, n):
        return pgb[p0:p0 + n]

    ident_f = sb("ident_f", [128, 128], F32); b_ident_f = Buf("ident_f")
    ident_b = sb("ident_b", [128, 128], BF16); b_ident_b = Buf("ident_b")
    cmask = sb("cmask", [128, 128], F32); b_cmask = Buf("cmask")
    smask = sb("smask", [128, 512], F32); b_smask = Buf("smask")
    lohi = sb("lohi", [128, 2], F32); b_lohi = Buf("lohi")
    normw = sb("normw", [128, 32], F32); b_normw = Buf("normw")
    lw = sb("lw", [128, 16], F32); b_lw = Buf("lw")
    hgw = sb("hgw", [128, 16], F32); b_hgw = Buf("hgw")
    l0 = sb("l0", [128, 16], F32); b_l0 = Buf("l0")
    l1 = sb("l1", [128, 16], F32); b_l1 = Buf("l1")
    lbt = sb("lbt", [128, 16], F32); b_lbt = Buf("lbt")
    oml = sb("oml", [128, 16], F32); b_oml = Buf("oml")
    ones_f = sb("ones_f", [128, 128], BF16); b_ones = Buf("ones")
    WmT = sb("WmT", [128, 16, 128], BF16); b_WmT = Buf("WmT")
    C2 = sb("C2", [128, 16, 128], F32); b_C2 = Buf("C2")
    S_f = sb("S_f", [128, 16, 128], F32); b_Sf = [Buf("Sf%d" % h) for h in range(16)]
    S_b = sb("S_b", [128, 16, 128], BF16); b_Sb = [Buf("Sb%d" % h) for h in range(16)]
    NSLOT = 3
    Wt = [sb("W%d" % i, [128, 32, GW], BF16) for i in range(NSLOT)]
    b_W = [Buf("W%d" % i) for i in range(NSLOT)]
    stat = sb("stat", [128, 64], F32)
    b_stat = [Buf("stat%d" % i) for i in range(64)]
    bnst = sb("bnst", [128, 4, 6], F32); b_bnst = Buf("bnst")
    mv = sb("mv", [128, 2], F32); b_mv = Buf("mv")

    NBANK = 8
    psum = [nc.alloc_psum_tensor("ps%d" % i, [128, 512], F32) for i in range(NBANK)]
    b_psh = [[Buf("ps%d_%d" % (i, hh)) for hh in range(2)] for i in range(NBANK)]
    b_ps = b_psh
    bank_ctr = [0]

    def bank():
        i = bank_ctr[0] % NBANK
        bank_ctr[0] += 1
        return i

    stat_ctr = [0]

    def stat_slot():
        i = stat_ctr[0] % 64
        stat_ctr[0] += 1
        return i

    wctr = [0]
    NGRP = 128
    wbf = nc.dram_tensor("wbf", [NGRP, 128, 32 * GW], BF16, kind="Internal").ap()
    b_wbf = [Buf("wbf%d" % i) for i in range(NGRP)]
    converted = set()

    def load_w(src_ap, gid, nk=32):
        s = wctr[0] % NSLOT
        wctr[0] += 1
        if gid in converted:
            src = wbf[gid, :, 0:nk * GW].rearrange("p (k c) -> p k c", c=GW)
            P.op("sp", lambda e, s=s, src=src, nk=nk: e.dma_start(out=Wt[s][:, 0:nk, :], in_=src),
                 reads=[b_wbf[gid]], writes=[b_W[s]], dma="w%d" % s)
        else:
            src = src_ap.rearrange("(k p) c -> p k c", p=128)
            P.op("pool", lambda e, s=s, src=src, nk=nk: e.dma_start(out=Wt[s][:, 0:nk, :], in_=src),
                 writes=[b_W[s]], dma="w%d" % s)
            if REUSE_BF16:
                dst = wbf[gid, :, 0:nk * GW].rearrange("p (k c) -> p k c", c=GW)
                P.op("sp", lambda e, s=s, dst=dst, nk=nk: e.dma_start(out=dst, in_=Wt[s][:, 0:nk, :]),
                     reads=[b_W[s]], writes=[b_wbf[gid]], dma="wb%d" % s)
                converted.add(gid)
        return s

    pcctr = [0]
    b_pc = [Buf("pc%d" % i) for i in range(4)]

    def preconvert(src_ap, gid, nk=32):
        if gid in converted:
            return
        i = pcctr[0] % 4
        pcctr[0] += 1
        src = src_ap.rearrange("(k p) c -> p k c", p=128)
        dst = wbf[gid, :, 0:nk * GW].rearrange("p (k c) -> p k c", c=GW)
        P.op("pool", lambda e, src=src, dst=dst: e.dma_start(out=dst, in_=src), writes=[b_wbf[gid], b_pc[i]], dma="pc%d" % i)
        converted.add(gid)

    def preconvert_plan(n):
        order = []
        for nm in ("u", "v", "za"):
            for g in range(8):
                order.append((nm, g))
        for hp in range(8):
            for nm in ("q", "og", "zb"):
                order.append((nm, hp))
        for nm, g in order[:n]:
            preconvert(w_in[:, COL[nm] + g * GW: COL[nm] + (g + 1) * GW], COL[nm] // GW + g)

    def preconvert_tail():
        for dg in range(16):
            preconvert(w_in[:, COL["ga"] + dg * GW: COL["ga"] + (dg + 1) * GW], COL["ga"] // GW + dg)
            preconvert(w_in[:, COL["gb"] + dg * GW: COL["gb"] + (dg + 1) * GW], COL["gb"] // GW + dg)
            preconvert(w_br[:, dg * GW:(dg + 1) * GW], 96 + dg)
        for og in range(16):
            preconvert(w_out[:, og * GW:(og + 1) * GW], 112 + og)

    def ld(dst, src, bufs, q="sp", key="ld_c"):
        P.op(q, lambda e: e.dma_start(out=dst, in_=src), writes=bufs, dma=key)

    ld(ident_f[:], c_ident, [b_ident_f], key="cc1")
    ld(cmask[:], c_cmask, [b_cmask], key="cc2")
    ld(lohi[:], c_lohi, [b_lohi], key="cc_lohi")
    ld(smask[:], c_smask, [b_smask], key="cc3")
    ld(normw[:], p_normw, [b_normw], key="cc4")
    ld(lw[:], p_lw, [b_lw], key="cc5")
    ld(hgw[:], p_hgw, [b_hgw], key="cc6")
    ld(l0[:], p_l0, [b_l0], key="cc7")
    ld(l1[:], p_l1, [b_l1], key="cc8")
    SC = 64
    ws_raw = A32(SC, 8)
    bs_col = A32(SC + 8, 1)[:, 0:16]
    lnb_bc = A32(SC + 16, 8)
    gmask = A32(SC + 24, 1)[:, 0:128]
    wm_tmp = A32(SC + 25, 1)[:, 0:128]
    wmT_f = A32(SC + 26, 1)[:, 0:128]
    cs_col = A32(SC + 27, 1)[:, 0:1]
    c2t = A32(SC + 28, 1)[:, 0:128]
    ld(ws_raw, p_ws, PB(SC, 8), key="cc9")
    ld(bs_col, p_bs, PB(SC + 8, 1), key="cc10")
    ld(lnb_bc, p_lnb, PB(SC + 16, 8), key="cc11")
    ld(gmask, c_gmask, PB(SC + 24, 1), key="cc12")

    P.op("dve", lambda e: e.memset(ones_f[:], 1.0), writes=[b_ones])
    P.op("dve", lambda e: e.memset(S_f[:], 0.0), writes=b_Sf)
    P.op("dve", lambda e: e.memset(S_b[:], 0.0), writes=b_Sb)
    P.op("dve", lambda e: e.tensor_copy(out=ident_b[:], in_=ident_f[:]), reads=[b_ident_f], writes=[b_ident_b])
    P.op("dve", lambda e: e.tensor_tensor(out=lbt[:], in0=l1[:], in1=l0[:], op=ALU.subtract), reads=[b_l0, b_l1], writes=[b_lbt])
    P.op("act", lambda e: e.activation(out=oml[:], in_=lbt[:], func=AF.Sigmoid), reads=[b_lbt], writes=[b_oml])

    for g in range(16):
        P.op("dve", lambda e, g=g: e.tensor_tensor(out=wm_tmp, in0=ws_raw[:, g * 128:(g + 1) * 128], in1=gmask, op=ALU.mult),
             reads=PB(SC, 8) + PB(SC + 24, 1), writes=PB(SC + 25, 1))
        bk = bank()
        P.op("pe", lambda e, bk=bk: e.transpose(out=psum[bk][:, 0:128], in_=wm_tmp, identity=ident_f[:]),
             reads=PB(SC + 25, 1) + [b_ident_f], writes=[b_ps[bk]])
        P.op("act", lambda e, bk=bk, g=g: e.activation(out=WmT[:, g, :], in_=psum[bk][:, 0:128], func=AF.Copy),
             reads=[b_ps[bk]], writes=[b_WmT])
        P.op("dve", lambda e: e.tensor_reduce(out=cs_col, in_=wm_tmp, axis=mybir.AxisListType.X, op=ALU.add),
             reads=PB(SC + 25, 1), writes=PB(SC + 27, 1))
        P.op("dve", lambda e, g=g: e.tensor_scalar(out=c2t, in0=lnb_bc[:, g * 128:(g + 1) * 128], scalar1=cs_col,
                                                    scalar2=bs_col[:, g:g + 1], op0=ALU.mult, op1=ALU.add),
             reads=PB(SC + 16, 8) + PB(SC + 27, 1) + PB(SC + 8, 1), writes=PB(SC + 28, 1))
        bk3 = bank()
        P.op("pe", lambda e, bk3=bk3: e.transpose(out=psum[bk3][:, 0:128], in_=c2t, identity=ident_f[:]),
             reads=PB(SC + 28, 1) + [b_ident_f], writes=[b_ps[bk3]])
        P.op("act", lambda e, bk3=bk3, g=g: e.activation(out=C2[:, g, :], in_=psum[bk3][:, 0:128], func=AF.Copy),
             reads=[b_ps[bk3]], writes=[b_C2])

    HT0, HA0, HB0 = 0, 32, 48
    hT = A16(HT0, 32).rearrange("p (k t) -> p k t", t=T)
    hA = A16(HA0, 16).rearrange("p (k t) -> p k t", t=T)
    hB = A16(HB0, 16).rearrange("p (k t) -> p k t", t=T)

    out_toks = []

    def phase0(src, r0):
        for blk in range(4):
            xp0 = SC + 16 * (blk % 2)
            xb = A32(xp0, 16)
            junk = A16(SC + 32, 8)
            P.op("sp", lambda e, xb=xb, r=r0 + blk * 128: e.dma_start(out=xb, in_=src[r:r + 128, :]),
                 writes=PB(xp0, 16), dma="x%d" % (blk % 2))
            s0 = stat_slot(); s1 = stat_slot(); s2 = stat_slot()
            P.op("act", lambda e, xb=xb, s0=s0: e.activation(out=junk, in_=xb, func=AF.Square, accum_out=stat[:, s0:s0 + 1]),
                 reads=PB(xp0, 16), writes=PB(SC + 32, 8) + [b_stat[s0]])
            P.op("dve", lambda e, s0=s0, s1=s1: e.tensor_scalar(out=stat[:, s1:s1 + 1], in0=stat[:, s0:s0 + 1], scalar1=1.0 / D,
                                                                 scalar2=EPS, op0=ALU.mult, op1=ALU.add),
                 reads=[b_stat[s0]], writes=[b_stat[s1]])
            P.op("act", lambda e, s1=s1, s2=s2: e.activation(out=stat[:, s2:s2 + 1], in_=stat[:, s1:s1 + 1], func=AF.Ln),
                 reads=[b_stat[s1]], writes=[b_stat[s2]])
            P.op("act", lambda e, s1=s1, s2=s2: e.activation(out=stat[:, s1:s1 + 1], in_=stat[:, s2:s2 + 1], func=AF.Exp, scale=-0.5),
                 reads=[b_stat[s2]], writes=[b_stat[s1]])
            P.op("dve", lambda e, xb=xb, s1=s1: e.tensor_scalar(out=xb, in0=xb, scalar1=stat[:, s1:s1 + 1], scalar2=None, op0=ALU.mult),
                 reads=PB(xp0, 16) + [b_stat[s1]], writes=PB(xp0, 16))
            for kb in range(8):
                bk = bank()

                def tr(e, bk=bk, kb=kb, xb=xb):
                    ins = None
                    for j in range(4):
                        kc = kb * 4 + j
                        ins = e.transpose(out=psum[bk][:, j * 128:(j + 1) * 128], in_=xb[:, kc * 128:(kc + 1) * 128],
                                          identity=ident_f[:])
                    return ins
                P.op("pe", tr, reads=PB(xp0, 16) + [b_ident_f], writes=[b_ps[bk]])
                for j in range(4):
                    kc = kb * 4 + j
                    if False:
                        pass
                    else:
                        P.op("dve", lambda e, bk=bk, j=j, kc=kc, blk=blk: e.tensor_scalar(
                            out=hT[:, kc, blk * 128:(blk + 1) * 128], in0=psum[bk][:, j * 128:(j + 1) * 128],
                            scalar1=normw[:, kc:kc + 1], scalar2=None, op0=ALU.mult),
                            reads=[b_ps[bk], b_normw], writes=PB(HT0 + kc, 1))

    def proj_fm(s, nk, rhs_of, rhs_bufs, koff=0):
        bks = []
        for cb in range(2):
            bk = bank()
            bks.append(bk)

            def mm(e, bk=bk, cb=cb):
                ins = None
                for kc in range(nk):
                    ins = e.matmul(psum[bk][:, :], lhsT=Wt[s][:, koff + kc, cb * 128:(cb + 1) * 128], rhs=rhs_of(kc),
                                   start=(kc == 0), stop=(kc == nk - 1))
                return ins
            P.op("pe", mm, reads=[b_W[s]] + rhs_bufs, writes=[b_ps[bk]])
        return bks

    def proj_tm(s, lhs_of, lhs_bufs):
        res = []
        bk = None
        for blk in range(4):
            bk = bank()
            c0 = 0

            def mm(e, bk=bk, blk=blk, c0=c0):
                ins = None
                for kc in range(32):
                    ins = e.matmul(psum[bk][:, c0:c0 + GW], lhsT=lhs_of(kc, blk), rhs=Wt[s][:, kc, :],
                                   start=(kc == 0), stop=(kc == 31))
                return ins
            P.op("pe", mm, reads=[b_W[s]] + lhs_bufs, writes=[b_ps[bk]])
            res.append((bk, c0))
        return res

    hT_bufs = PB(HT0, 32)

    def hT_rhs(kc):
        return hT[:, kc, :]

    def hT_lhs(kc, blk):
        return hT[:, kc, blk * 128:(blk + 1) * 128]

    def phase1():
        for ug in range(8):
            s = load_w(w_in[:, COL["u"] + ug * GW: COL["u"] + (ug + 1) * GW], COL["u"] // GW + ug)
            bks = proj_fm(s, 32, hT_rhs, hT_bufs)
            for cb, bk in enumerate(bks):
                g = ug * 2 + cb
                P.op("act", lambda e, bk=bk, g=g: e.activation(out=hA[:, g, :], in_=psum[bk][:, :], func=AF.Gelu_apprx_tanh),
                     reads=[b_ps[bk]], writes=PB(HA0 + g, 1))
        gv = A16(SC, 16).rearrange("p (b c) -> p b c", c=2048)
        for vg in range(8):
            s = load_w(w_in[:, COL["v"] + vg * GW: COL["v"] + (vg + 1) * GW], COL["v"] // GW + vg)
            res = proj_tm(s, hT_lhs, hT_bufs)
            for blk, (bk, c0) in enumerate(res):
                P.op("act", lambda e, bk=bk, c0=c0, blk=blk, vg=vg: e.activation(
                    out=gv[:, blk, vg * GW:(vg + 1) * GW], in_=psum[bk][:, c0:c0 + GW], func=AF.Gelu_apprx_tanh),
                    reads=[b_ps[bk]], writes=PB(SC + blk * 4 + vg // 2, 1))
        for blk in range(4):
            for c in range(4):
                P.op("dve", lambda e, blk=blk, c=c: e.bn_stats(out=bnst[:, c, :], in_=gv[:, blk, c * 512:(c + 1) * 512]),
                     reads=PB(SC + blk * 4 + c, 1), writes=[b_bnst])
            P.op("dve", lambda e: e.bn_aggr(out=mv[:], in_=bnst[:].rearrange("p a b -> p (a b)")), reads=[b_bnst], writes=[b_mv])
            s1 = stat_slot(); s2 = stat_slot(); s3 = stat_slot()
            P.op("dve", lambda e, s1=s1: e.tensor_scalar(out=stat[:, s1:s1 + 1], in0=mv[:, 1:2], scalar1=EPS, scalar2=None, op0=ALU.add),
                 reads=[b_mv], writes=[b_stat[s1]])
            P.op("act", lambda e, s1=s1, s2=s2: e.activation(out=stat[:, s2:s2 + 1], in_=stat[:, s1:s1 + 1], func=AF.Ln),
                 reads=[b_stat[s1]], writes=[b_stat[s2]])
            P.op("act", lambda e, s1=s1, s2=s2: e.activation(out=stat[:, s1:s1 + 1], in_=stat[:, s2:s2 + 1], func=AF.Exp, scale=-0.5),
                 reads=[b_stat[s2]], writes=[b_stat[s1]])
            P.op("dve", lambda e, s1=s1, s3=s3: e.tensor_scalar(out=stat[:, s3:s3 + 1], in0=mv[:, 0:1], scalar1=stat[:, s1:s1 + 1],
                                                                 scalar2=-1.0, op0=ALU.mult, op1=ALU.mult),
                 reads=[b_mv, b_stat[s1]], writes=[b_stat[s3]])
            P.op("dve", lambda e, blk=blk, s1=s1, s3=s3: e.tensor_scalar(
                out=gv[:, blk, :], in0=gv[:, blk, :], scalar1=stat[:, s1:s1 + 1], scalar2=stat[:, s3:s3 + 1],
                op0=ALU.mult, op1=ALU.add),
                reads=PB(SC + blk * 4, 4) + [b_stat[s1], b_stat[s3]], writes=PB(SC + blk * 4, 4))
        for g in range(16):
            bk = bank()

            def mm(e, bk=bk, g=g):
                ins = None
                for blk in range(4):
                    ins = e.matmul(psum[bk][:, blk * 128:(blk + 1) * 128], lhsT=gv[:, blk, g * 128:(g + 1) * 128],
                                   rhs=WmT[:, g, :], start=True, stop=True)
                return ins
            P.op("pe", mm, reads=PB(SC, 16) + [b_WmT], writes=[b_ps[bk]])
            tp = SC + 16 + 2 * (g % 2)
            t1 = A32(tp, 2)
            P.op("dve", lambda e, bk=bk, g=g, t1=t1: e.scalar_tensor_tensor(
                out=t1.rearrange("p (b i) -> p b i", i=128), in0=psum[bk][:, :].rearrange("p (b i) -> p b i", i=128),
                scalar=lw[:, g:g + 1], in1=C2[:, g:g + 1, :].broadcast_to([128, 4, 128]), op0=ALU.mult, op1=ALU.add),
                reads=[b_ps[bk], b_lw, b_C2], writes=PB(tp, 2))
            P.op("dve", lambda e, g=g, t1=t1: e.tensor_tensor(out=hA[:, g, :], in0=hA[:, g, :], in1=t1, op=ALU.mult),
                 reads=PB(tp, 2) + PB(HA0 + g, 1), writes=PB(HA0 + g, 1))
        for zg in range(8):
            s = load_w(w_in[:, COL["za"] + zg * GW: COL["za"] + (zg + 1) * GW], COL["za"] // GW + zg)
            bks = proj_fm(s, 32, hT_rhs, hT_bufs)
            for cb, bk in enumerate(bks):
                g = zg * 2 + cb
                tp = SC + 20 + 2 * (g % 2)
                t1 = A32(tp, 2)
                P.op("act", lambda e, bk=bk, t1=t1: e.activation(out=t1, in_=psum[bk][:, :], func=AF.Silu),
                     reads=[b_ps[bk]], writes=PB(tp, 2))
                P.op("dve", lambda e, g=g, t1=t1: e.tensor_tensor(out=hA[:, g, :], in0=hA[:, g, :], in1=t1, op=ALU.mult),
                     reads=PB(tp, 2) + PB(HA0 + g, 1), writes=PB(HA0 + g, 1))

    INP0 = SC
    inpT = A16(INP0, 16).rearrange("p (b c) -> p b c", c=2048)

    def head_temps(hl):
        base = SC + 16 + hl * 24
        d = {}
        names = [("tk", 2), ("tq", 2), ("tf", 2), ("tc", 2), ("kp", 1), ("kd", 2), ("kdT", 2), ("qd", 1), ("sT", 1),
                 ("ver", 2), ("sog", 1), ("szb", 1), ("oc", 2), ("osq", 2)]
        p = base
        for n, k in names:
            d[n] = (p, k)
            p += k
        assert p <= base + 24
        d["Sv"] = (base, 4)
        return d

    class HC:
        def __init__(self, hl):
            self.tm = head_temps(hl)

        def v32(self, n):
            return A32(*self.tm[n])

        def v16(self, n):
            return A16(*self.tm[n])

        def b(self, n):
            return PB(*self.tm[n])

    def evac_f(hl, bk):
        c = HC(hl)
        P.op("act", lambda e, bk=bk, tk=c.v32("tk"): e.activation(out=tk, in_=psum[bk][:, :], func=AF.Sigmoid, scale=-1.0),
             reads=[b_ps[bk]], writes=c.b("tk"))

    def fchain(h, hl):
        c = HC(hl)
        tk, tf, tc = c.v32("tk"), c.v32("tf"), c.v32("tc")
        kp, kd = c.v16("kp"), c.v32("kd")
        P.op("dve", lambda e: e.tensor_scalar(out=tk, in0=tk, scalar1=oml[:, h:h + 1], scalar2=None, op0=ALU.mult),
             reads=c.b("tk") + [b_oml], writes=c.b("tk"))
        P.op("dve", lambda e: e.tensor_scalar(out=tf, in0=tk, scalar1=-1.0, scalar2=1.0, op0=ALU.mult, op1=ALU.add),
             reads=c.b("tk"), writes=c.b("tf"))
        P.op("act", lambda e: e.activation(out=tf, in_=tf, func=AF.Ln), reads=c.b("tf"), writes=c.b("tf"))
        P.op("dve", lambda e: e.tensor_tensor_scan(out=tc, data0=smask[:], data1=tf, initial=0.0, op0=ALU.mult, op1=ALU.add),
             reads=c.b("tf") + [b_smask], writes=c.b("tc"))
        P.op("act", lambda e: e.activation(out=tf, in_=tc, func=AF.Exp), reads=c.b("tc"), writes=c.b("tf"))
        P.op("act", lambda e: e.activation(out=tc, in_=tc, func=AF.Exp, scale=-1.0), reads=c.b("tc"), writes=c.b("tc"))
        P.op("dve", lambda e: e.tensor_tensor(out=kp, in0=tk, in1=tc, op=ALU.mult),
             reads=c.b("tk") + c.b("tc"), writes=c.b("kp"))
        tA3 = tf.rearrange("p (n i) -> p n i", i=64)
        P.op("dve", lambda e: e.tensor_tensor(
            out=kd.rearrange("p (n i) -> p n i", i=64), in0=kp.rearrange("p (n i) -> p n i", i=64),
            in1=tA3[:, :, 63:64].broadcast_to([128, 8, 64]), op=ALU.mult),
            reads=c.b("kp") + c.b("tf"), writes=c.b("kd"))

    def kdt_stage(h, hl):
        c = HC(hl)
        kd, kdT = c.v32("kd"), c.v16("kdT")
        bkT = bank()

        def trk(e):
            ins = None
            for blk in range(4):
                ins = e.transpose(out=psum[bkT][:, blk * 128:(blk + 1) * 128], in_=kd[:, blk * 128:(blk + 1) * 128],
                                  identity=ident_f[:])
            return ins
        P.op("pe", trk, reads=c.b("kd") + [b_ident_f], writes=[b_ps[bkT]])
        P.op("act", lambda e: e.activation(out=kdT[:, 0:512], in_=psum[bkT][:, :], func=AF.Identity, scale=lohi[:, 0:1]),
             reads=[b_ps[bkT], b_lohi], writes=c.b("kdT"))
        P.op("dve", lambda e: e.tensor_scalar(out=kdT[:, 512:1024], in0=psum[bkT][:, :], scalar1=lohi[:, 1:2], scalar2=None,
                                              op0=ALU.mult),
             reads=[b_ps[bkT], b_lohi], writes=c.b("kdT"))

    def state_stage(h, hl, own):
        c = HC(hl)
        kdT, tf = c.v16("kdT"), c.v32("tf")
        bkU = [bank(), bank()]

        def umm(e):
            ins = None
            for n in range(8):
                blk, ch = n // 2, n % 2
                ins = e.matmul(psum[bkU[n // 4]][:, (n % 4) * 128:(n % 4 + 1) * 128],
                               lhsT=kdT[:, ch * 512 + blk * 128:ch * 512 + (blk + 1) * 128],
                               rhs=inpT[:, blk, h * 128:(h + 1) * 128], start=True, stop=True)
            return ins
        P.op("pe", umm, reads=c.b("kdT") + PB(INP0, 16), writes=[b_ps[bkU[0]], b_ps[bkU[1]]])
        Sv = A32(*c.tm["Sv"]).rearrange("p (n v) -> p n v", v=128)
        ver = c.v16("ver").rearrange("p (n v) -> p n v", v=128)
        if own:
            P.op("act", lambda e: e.activation(out=ver[:, 0, :], in_=S_b[:, h, :], func=AF.Copy),
                 reads=[b_Sb[h]], writes=c.b("ver"))
        for n in range(8):
            prev = S_f[:, h, :] if n == 0 else Sv[:, n - 1, :]
            P.op("dve", lambda e, n=n, prev=prev: e.scalar_tensor_tensor(
                out=Sv[:, n, :], in0=prev, scalar=tf[:, n * 64 + 63:n * 64 + 64],
                in1=psum[bkU[n // 4]][:, (n % 4) * 128:(n % 4 + 1) * 128], op0=ALU.mult, op1=ALU.add),
                reads=[b_Sf[h], b_ps[bkU[n // 4]]] + c.b("tf") + c.b("Sv"), writes=c.b("Sv"))
        if own:
            P.op("act", lambda e: e.activation(out=ver[:, 1:8, :], in_=Sv[:, 0:7, :], func=AF.Copy),
                 reads=c.b("Sv"), writes=c.b("ver"))
        P.op("act", lambda e: e.activation(out=S_b[:, h, :], in_=Sv[:, 7, :], func=AF.Copy), reads=c.b("Sv"), writes=[b_Sb[h]])
        P.op("dve", lambda e: e.tensor_copy(out=S_f[:, h, :], in_=Sv[:, 7, :]), reads=c.b("Sv"), writes=[b_Sf[h]])

    def score_stage(h, hl):
        c = HC(hl)
        kp, qd, sT = c.v16("kp"), c.v16("qd"), c.v16("sT")
        bkS = bank()

        def smm(e):
            ins = None
            for blk in range(4):
                ins = e.matmul(psum[bkS][:, blk * 128:(blk + 1) * 128], lhsT=kp[:, blk * 128:(blk + 1) * 128],
                               rhs=qd[:, blk * 128:(blk + 1) * 128], start=True, stop=True)
            return ins
        P.op("pe", smm, reads=c.b("kp") + c.b("qd"), writes=[b_ps[bkS]])
        P.op("dve", lambda e: e.tensor_tensor(
            out=sT.rearrange("p (b i) -> p b i", i=128), in0=psum[bkS][:, :].rearrange("p (b i) -> p b i", i=128),
            in1=cmask[:].rearrange("p (o i) -> p o i", o=1).broadcast_to([128, 4, 128]), op=ALU.mult),
            reads=[b_ps[bkS], b_cmask], writes=c.b("sT"))

    def out_stage(h, hl):
        c = HC(hl)
        qd, sT = c.v16("qd"), c.v16("sT")
        ver = c.v16("ver").rearrange("p (n v) -> p n v", v=128)
        bkO = bank()
        bkI = bank()

        def omm(e):
            ins = None
            for blk in range(4):
                e.matmul(psum[bkO][:, blk * 128:(blk + 1) * 128], lhsT=inpT[:, blk, h * 128:(h + 1) * 128],
                         rhs=sT[:, blk * 128:(blk + 1) * 128], start=True, stop=True)
            for n in range(8):
                c0 = n * 64
                ins = e.matmul(psum[bkI][:, c0:c0 + 64], lhsT=ver[:, n, :], rhs=qd[:, c0:c0 + 64], start=True, stop=True)
            return ins
        P.op("pe", omm, reads=c.b("sT") + c.b("qd") + c.b("ver") + PB(INP0, 16), writes=[b_ps[bkO], b_ps[bkI]])
        oc, osq = c.v32("oc"), c.v32("osq")
        osqb = c.v16("osq")[:, 0:512]
        P.op("act", lambda e: e.activation(out=oc, in_=psum[bkI][:, :], func=AF.Copy), reads=[b_ps[bkI]], writes=c.b("oc"))
        P.op("dve", lambda e: e.tensor_tensor(out=oc, in0=psum[bkO][:, :], in1=oc, op=ALU.add),
             reads=[b_ps[bkO]] + c.b("oc"), writes=c.b("oc"))
        P.op("act", lambda e: e.activation(out=osqb, in_=oc, func=AF.Square), reads=c.b("oc"), writes=c.b("osq"))

    def norm_stage(h, hl):
        c = HC(hl)
        oc, osq = c.v32("oc"), c.v32("osq")
        osqb = c.v16("osq")[:, 0:512]
        sog, szb = c.v16("sog"), c.v16("szb")
        bkN = bank()
        P.op("pe", lambda e: e.matmul(psum[bkN][:, :], lhsT=ones_f[:], rhs=osqb, start=True, stop=True),
             reads=c.b("osq") + [b_ones], writes=[b_ps[bkN]])
        P.op("dve", lambda e: e.tensor_scalar(out=osq, in0=psum[bkN][:, :], scalar1=1.0 / 128, scalar2=EPS, op0=ALU.mult, op1=ALU.add),
             reads=[b_ps[bkN]], writes=c.b("osq"))
        P.op("act", lambda e: e.activation(out=osq, in_=osq, func=AF.Ln), reads=c.b("osq"), writes=c.b("osq"))
        P.op("act", lambda e: e.activation(out=osq, in_=osq, func=AF.Exp, scale=-0.5), reads=c.b("osq"), writes=c.b("osq"))
        P.op("dve", lambda e: e.tensor_tensor(out=oc, in0=oc, in1=osq, op=ALU.mult), reads=c.b("oc") + c.b("osq"), writes=c.b("oc"))
        P.op("dve", lambda e: e.tensor_tensor(out=oc, in0=oc, in1=sog, op=ALU.mult), reads=c.b("oc") + c.b("sog"), writes=c.b("oc"))
        P.op("dve", lambda e: e.scalar_tensor_tensor(out=hB[:, h, :], in0=oc, scalar=hgw[:, h:h + 1], in1=szb,
                                                     op0=ALU.mult, op1=ALU.mult),
             reads=c.b("oc") + c.b("szb") + [b_hgw], writes=PB(HB0 + h, 1))

    def wproj(name, hp):
        s = load_w(w_in[:, COL[name] + hp * GW: COL[name] + (hp + 1) * GW], COL[name] // GW + hp)
        return proj_fm(s, 32, hT_rhs, hT_bufs)

    def phase2(own):
        for ig in range(8):
            s = load_w(w_in[:, COL["inp"] + ig * GW: COL["inp"] + (ig + 1) * GW], COL["inp"] // GW + ig)
            res = proj_tm(s, hT_lhs, hT_bufs)
            for blk, (bk, c0) in enumerate(res):
                P.op("act", lambda e, bk=bk, c0=c0, blk=blk, ig=ig: e.activation(
                    out=inpT[:, blk, ig * GW:(ig + 1) * GW], in_=psum[bk][:, c0:c0 + GW], func=AF.Copy),
                    reads=[b_ps[bk]], writes=PB(INP0 + blk * 4 + ig // 2, 1))
        if not own:
            bkF = wproj("f", 0)
            for hl in range(2):
                evac_f(hl, bkF[hl])
            for hl in range(2):
                fchain(hl, hl)
            for hp in range(8):
                if hp + 1 < 8:
                    bkF = wproj("f", hp + 1)
                for hl in range(2):
                    kdt_stage(hp * 2 + hl, hl)
                for hl in range(2):
                    state_stage(hp * 2 + hl, hl, False)
                if hp + 1 < 8:
                    for hl in range(2):
                        evac_f(hl, bkF[hl])
                    for hl in range(2):
                        fchain((hp + 1) * 2 + hl, hl)
            return
        for hp in range(8):
            bkF = wproj("f", hp)
            for hl in range(2):
                evac_f(hl, bkF[hl])
            for hl in range(2):
                fchain(hp * 2 + hl, hl)
            bkQ = wproj("q", hp)
            for hl in range(2):
                c = HC(hl)
                P.op("act", lambda e, bq=bkQ[hl], tq=c.v32("tq"): e.activation(out=tq, in_=psum[bq][:, :], func=AF.Silu),
                     reads=[b_ps[bkQ[hl]]], writes=c.b("tq"))
            for hl in range(2):
                c = HC(hl)
                P.op("dve", lambda e, tq=c.v32("tq"), tf=c.v32("tf"), qd=c.v16("qd"): e.tensor_tensor(out=qd, in0=tq, in1=tf, op=ALU.mult),
                     reads=c.b("tq") + c.b("tf"), writes=c.b("qd"))
            bkO2 = wproj("og", hp)
            for hl in range(2):
                c = HC(hl)
                P.op("act", lambda e, b=bkO2[hl], t=c.v16("sog"): e.activation(out=t, in_=psum[b][:, :], func=AF.Sigmoid),
                     reads=[b_ps[bkO2[hl]]], writes=c.b("sog"))
            bkZ = wproj("zb", hp)
            for hl in range(2):
                c = HC(hl)
                P.op("act", lambda e, b=bkZ[hl], t=c.v16("szb"): e.activation(out=t, in_=psum[b][:, :], func=AF.Silu),
                     reads=[b_ps[bkZ[hl]]], writes=c.b("szb"))
            for hl in range(2):
                kdt_stage(hp * 2 + hl, hl)
            for hl in range(2):
                score_stage(hp * 2 + hl, hl)
            for hl in range(2):
                state_stage(hp * 2 + hl, hl, True)
            for hl in range(2):
                out_stage(hp * 2 + hl, hl)
            for hl in range(2):
                norm_stage(hp * 2 + hl, hl)

    MG0 = SC
    merged = A16(MG0, 32).rearrange("p (k t) -> p k t", t=T)

    def phase3():
        for dg in range(16):
            sA = load_w(w_in[:, COL["ga"] + dg * GW: COL["ga"] + (dg + 1) * GW], COL["ga"] // GW + dg)
            bkGA = proj_fm(sA, 32, hT_rhs, hT_bufs)
            sB = load_w(w_in[:, COL["gb"] + dg * GW: COL["gb"] + (dg + 1) * GW], COL["gb"] // GW + dg)
            bkGB = proj_fm(sB, 32, hT_rhs, hT_bufs)
            sW = load_w(w_br[:, dg * GW:(dg + 1) * GW], 96 + dg)
            bkPA = proj_fm(sW, 16, lambda kc: hA[:, kc, :], PB(HA0, 16), koff=0)
            bkPB = proj_fm(sW, 16, lambda kc: hB[:, kc, :], PB(HB0, 16), koff=16)
            for cb in range(2):
                dmb = dg * 2 + cb
                tp = SC + 32 + 4 * cb
                ta, tb = A32(tp, 2), A32(tp + 2, 2)
                P.op("act", lambda e, b=bkGA[cb], ta=ta: e.activation(out=ta, in_=psum[b][:, :], func=AF.Sigmoid),
                     reads=[b_ps[bkGA[cb]]], writes=PB(tp, 2))
                P.op("act", lambda e, b=bkGB[cb], tb=tb: e.activation(out=tb, in_=psum[b][:, :], func=AF.Sigmoid),
                     reads=[b_ps[bkGB[cb]]], writes=PB(tp + 2, 2))
                P.op("dve", lambda e, b=bkPA[cb], ta=ta: e.tensor_tensor(out=ta, in0=psum[b][:, :], in1=ta, op=ALU.mult),
                     reads=[b_ps[bkPA[cb]]] + PB(tp, 2), writes=PB(tp, 2))
                P.op("dve", lambda e, b=bkPB[cb], tb=tb: e.tensor_tensor(out=tb, in0=psum[b][:, :], in1=tb, op=ALU.mult),
                     reads=[b_ps[bkPB[cb]]] + PB(tp + 2, 2), writes=PB(tp + 2, 2))
                P.op("dve", lambda e, ta=ta, tb=tb, dmb=dmb: e.tensor_tensor(out=merged[:, dmb, :], in0=ta, in1=tb, op=ALU.add),
                     reads=PB(tp, 4), writes=PB(MG0 + dmb, 1))

    Y0 = 0
    yv = A32(Y0, 64).rearrange("p (b c) -> p b c", c=D)
    FW0 = SC + 32
    fwb = A32(FW0, 16)

    def phase4(r0):
        for blk in range(4):
            P.op("sp", lambda e, blk=blk: e.dma_start(out=yv[:, blk, :], in_=x[r0 + blk * 128:r0 + (blk + 1) * 128, :]),
                 writes=PB(Y0 + blk * 16, 16), dma="y%d" % blk)
        P.op("sp", lambda e: e.dma_start(out=fwb, in_=p_fw), writes=PB(FW0, 16), dma="fw")
        for og in range(16):
            s = load_w(w_out[:, og * GW:(og + 1) * GW], 112 + og)
            res = proj_tm(s, lambda kc, blk: merged[:, kc, blk * 128:(blk + 1) * 128], PB(MG0, 32))
            for blk, (bk, c0) in enumerate(res):
                P.op("dve", lambda e, bk=bk, c0=c0, blk=blk, og=og: e.tensor_tensor(
                    out=yv[:, blk, og * GW:(og + 1) * GW], in0=psum[bk][:, c0:c0 + GW], in1=yv[:, blk, og * GW:(og + 1) * GW], op=ALU.add),
                    reads=[b_ps[bk]] + PB(Y0 + blk * 16 + og, 1), writes=PB(Y0 + blk * 16 + og, 1))
        junk = A16(MG0, 8)
        for blk in range(4):
            s0 = stat_slot(); s1 = stat_slot(); s2 = stat_slot()
            P.op("act", lambda e, blk=blk, s0=s0: e.activation(out=junk, in_=yv[:, blk, :], func=AF.Square, accum_out=stat[:, s0:s0 + 1]),
                 reads=PB(Y0 + blk * 16, 16), writes=PB(MG0, 8) + [b_stat[s0]])
            P.op("dve", lambda e, s0=s0, s1=s1: e.tensor_scalar(out=stat[:, s1:s1 + 1], in0=stat[:, s0:s0 + 1], scalar1=1.0 / D,
                                                                 scalar2=EPS, op0=ALU.mult, op1=ALU.add),
                 reads=[b_stat[s0]], writes=[b_stat[s1]])
            P.op("act", lambda e, s1=s1, s2=s2: e.activation(out=stat[:, s2:s2 + 1], in_=stat[:, s1:s1 + 1], func=AF.Ln),
                 reads=[b_stat[s1]], writes=[b_stat[s2]])
            P.op("act", lambda e, s1=s1, s2=s2: e.activation(out=stat[:, s1:s1 + 1], in_=stat[:, s2:s2 + 1], func=AF.Exp, scale=-0.5),
                 reads=[b_stat[s2]], writes=[b_stat[s1]])
            P.op("dve", lambda e, blk=blk, s1=s1: e.scalar_tensor_tensor(
                out=yv[:, blk, :], in0=yv[:, blk, :], scalar=stat[:, s1:s1 + 1], in1=fwb, op0=ALU.mult, op1=ALU.mult),
                reads=PB(Y0 + blk * 16, 16) + PB(FW0, 16) + [b_stat[s1]], writes=PB(Y0 + blk * 16, 16))
            tok = P.op("sp", lambda e, blk=blk: e.dma_start(out=out[r0 + blk * 128:r0 + (blk + 1) * 128, :], in_=yv[:, blk, :]),
                       reads=PB(Y0 + blk * 16, 16), dma="o%d" % blk)
            out_toks.append(tok)

    import os as _os
    stages = _os.environ.get("K_STAGES", "w0,w2,p0,p1,p2,p3,p4").split(",")
    for t in range(n_warm):
        if "w0" in stages:
            phase0(xp, t * T)
        if "w2" in stages:
            phase2(False)
        if t == 0 and n_warm > 1 and REUSE_BF16 and N_PRECONV > 0:
            preconvert_plan(N_PRECONV)
    for t in range(n_own):
        if t == 0 and n_warm > 1 and REUSE_BF16 and N_PRECONV >= 48:
            preconvert_tail()
        if "p0" in stages:
            phase0(x, t * T)
        if "p1" in stages:
            phase1()
        if "p2" in stages:
            phase2(True)
        if dbg and t == n_own - 1:
            out_toks.append(P.op("sp", lambda e: e.dma_start(out=d_hT, in_=A16(HT0, 32)), reads=PB(HT0, 32), dma="dbg0"))
            out_toks.append(P.op("sp", lambda e: e.dma_start(out=d_hA, in_=A16(HA0, 16)), reads=PB(HA0, 16), dma="dbg1"))
            out_toks.append(P.op("sp", lambda e: e.dma_start(out=d_hB, in_=A16(HB0, 16)), reads=PB(HB0, 16), dma="dbg2"))
        if "p3" in stages:
            phase3()
        if dbg and t == n_own - 1:
            out_toks.append(P.op("sp", lambda e: e.dma_start(out=d_mg, in_=A16(MG0, 32)), reads=PB(MG0, 32), dma="dbg3"))
        if "p4" in stages:
            phase4(t * T)
    P.wait_all("sp", out_toks)
    P.emit()
    return nc


def _consts():
    ident = np.eye(128, dtype=np.float32)
    idx = np.arange(128)
    ch = idx // 64
    cmask = ((ch[:, None] == ch[None, :]) & (idx[:, None] <= idx[None, :])).astype(np.float32)
    gmask = (ch[None, :] <= ch[:, None]).astype(np.float32)
    smask = np.ones((128, 512), np.float32)
    smask[:, ::64] = 0.0
    lohi = np.zeros((128, 2), np.float32)
    lohi[:64, 0] = 1.0
    lohi[64:, 1] = 1.0
    return ident, cmask, gmask, smask, lohi


def _param_maps(norm_w, gmlp_ln_w, gmlp_ln_b, gmlp_w_s, gmlp_b_s, hgrn_lb_logits, hgrn_norm_w, final_norm_w):
    ident, cmask, gmask, smask, lohi = _consts()
    f = np.float32
    return {
        "c_ident": ident, "c_cmask": cmask, "c_gmask": gmask, "c_smask": smask, "c_lohi": lohi,
        "p_normw": np.ascontiguousarray(norm_w[0].reshape(32, 128).T, dtype=f),
        "p_lw": np.ascontiguousarray(gmlp_ln_w[0].reshape(16, 128).T, dtype=f),
        "p_hgw": np.ascontiguousarray(hgrn_norm_w[0].reshape(16, 128).T, dtype=f),
        "p_l0": np.ascontiguousarray(hgrn_lb_logits[0].reshape(16, 128).T, dtype=f),
        "p_l1": np.ascontiguousarray(hgrn_lb_logits[1].reshape(16, 128).T, dtype=f),
        "p_fw": np.ascontiguousarray(np.broadcast_to(final_norm_w[None, :], (128, D)), dtype=f),
        "p_ws": np.ascontiguousarray(np.transpose(gmlp_w_s[0], (1, 0, 2)).reshape(128, 2048), dtype=f),
        "p_bs": np.ascontiguousarray(gmlp_b_s[0].T, dtype=f),
        "p_lnb": np.ascontiguousarray(np.broadcast_to(gmlp_ln_b[0][None, :], (128, 2048)), dtype=f),
    }


_NC_CACHE = {}


def kernel(x, norm_w, w_in, gmlp_ln_w, gmlp_ln_b, gmlp_w_s, gmlp_b_s, hgrn_lb_logits, hgrn_norm_w,
           w_branch, w_out, final_norm_w):
    x = np.asarray(x, dtype=np.float32)
    Bsz, S, _ = x.shape
    half = S // 2
    n_own = half // T
    n_warm = half // T
    key = (n_own, n_warm)
    if key not in _NC_CACHE:
        _NC_CACHE[key] = build(n_own, n_warm)
    nc = _NC_CACHE[key]
    pm = _param_maps(np.asarray(norm_w), np.asarray(gmlp_ln_w), np.asarray(gmlp_ln_b), np.asarray(gmlp_w_s),
                     np.asarray(gmlp_b_s), np.asarray(hgrn_lb_logits), np.asarray(hgrn_norm_w), np.asarray(final_norm_w))
    w_in0 = np.ascontiguousarray(np.asarray(w_in)[0], dtype=np.float32)
    w_br0 = np.ascontiguousarray(np.asarray(w_branch)[0].reshape(2 * 2048, D), dtype=np.float32)
    w_out0 = np.ascontiguousarray(np.asarray(w_out)[0], dtype=np.float32)
    zeros = np.zeros((half, D), np.float32)
    in_maps = []
    for c in range(8):
        b, hf = c // 2, c % 2
        m = dict(pm)
        m["x"] = np.ascontiguousarray(x[b, hf * half:(hf + 1) * half])
        m["xp"] = zeros if hf == 0 else np.ascontiguousarray(x[b, 0:half])
        m["w_in"] = w_in0
        m["w_br"] = w_br0
        m["w_out"] = w_out0
        in_maps.append(m)
    res = run_bass_kernel_spmd(nc, in_maps, core_ids=list(range(8)))
    outp = np.empty((Bsz, S, D), np.float32)
    for c in range(8):
        b, hf = c // 2, c % 2
        outp[b, hf * half:(hf + 1) * half] = res.results[c]["out"]
    return outp
```

```python
import numpy as np
import concourse.bass as bass
import concourse.mybir as mybir
from concourse.bass_utils import run_bass_kernel_spmd

F32 = mybir.dt.float32
BF16 = mybir.dt.bfloat16
AF = mybir.ActivationFunctionType
ALU = mybir.AluOpType

D = 4096
T = 512
GW = 256
EPS = 1e-6
COL = dict(u=0, v=2048, za=4096, q=6144, f=8192, inp=10240, og=12288, zb=14336, ga=16384, gb=20480)
NPG = 128
REUSE_BF16 = True
N_PRECONV = 48


class Buf:
    __slots__ = ("name", "last_w", "readers")

    def __init__(self, name):
        self.name = name
        self.last_w = None
        self.readers = {}


class Prog:
    ENGS = ("pe", "act", "dve", "pool", "sp")

    def __init__(self, nc):
        self.nc = nc
        self.ops = {e: [] for e in self.ENGS}
        self.sems = {}
        self.cnt = {}
        self.seen = {e: {} for e in self.ENGS}
        self.nops = {e: 0 for e in self.ENGS}
        for e in self.ENGS:
            self._mksem("c_" + e)

    def _mksem(self, key):
        self.sems[key] = self.nc.alloc_semaphore(name=key)
        self.cnt[key] = 0

    def _collect(self, eng, reads, writes):
        waits = {}

        def need(tok, raw):
            if tok is None:
                return
            key, val, teng, tidx = tok
            if teng == eng and key == "c_" + eng:
                if eng == "pe" or not raw:
                    return
                if tidx < self.nops[eng] - 2:
                    return
            if val > waits.get(key, 0):
                waits[key] = val

        for b in reads:
            need(b.last_w, True)
        for b in writes:
            need(b.last_w, False)
            for r in b.readers.values():
                need(r, False)
        out = []
        for key, val in waits.items():
            if self.seen[eng].get(key, 0) >= val:
                continue
            self.seen[eng][key] = val
            out.append((key, val))
        return out

    @staticmethod
    def _flat(xs):
        out = []
        for b in xs:
            if isinstance(b, (list, tuple)):
                out.extend(Prog._flat(b))
            else:
                out.append(b)
        return out

    def op(self, eng, fn, reads=(), writes=(), dma=None):
        reads = self._flat(reads)
        writes = self._flat(writes)
        waits = self._collect(eng, reads, writes)
        if dma is None:
            key = "c_" + eng
            inc = 1
        else:
            key = dma
            if key not in self.sems:
                self._mksem(key)
            inc = 16
        self.cnt[key] += inc
        val = self.cnt[key]
        tok = (key, val, eng, self.nops[eng])
        self.nops[eng] += 1
        sems = self.sems
        sem = sems[key]

        def run(e, waits=waits, fn=fn, sem=sem, inc=inc):
            for k, v in waits:
                e.wait_ge(sems[k], v)
            ins = fn(e)
            ins.then_inc(sem, inc)

        self.ops[eng].append(run)
        for b in writes:
            b.last_w = tok
            b.readers = {}
        for b in reads:
            b.readers[key] = tok
        return tok

    def wait_all(self, eng, toks):
        sems = self.sems
        ws = {}
        for key, val, _, _ in toks:
            ws[key] = max(ws.get(key, 0), val)

        def run(e, ws=ws):
            for k, v in ws.items():
                e.wait_ge(sems[k], v)

        self.ops[eng].append(run)

    def emit(self):
        nc = self.nc
        with nc.Block() as block:
            @block.tensor
            def _(e):
                for f in self.ops["pe"]:
                    f(e)

            @block.scalar
            def _(e):
                for f in self.ops["act"]:
                    f(e)

            @block.vector
            def _(e):
                for f in self.ops["dve"]:
                    f(e)

            @block.gpsimd
            def _(e):
                for f in self.ops["pool"]:
                    f(e)

            @block.sync
            def _(e):
                for f in self.ops["sp"]:
                    f(e)


def build(n_own, n_warm, dbg=False):
    nc = bass.Bass("TRN2", target_bir_lowering=False)

    def dram(name, shape, kind="ExternalInput"):
        return nc.dram_tensor(name, shape, F32, kind=kind).ap()

    x = dram("x", [n_own * T, D])
    xp = dram("xp", [max(n_warm, 1) * T, D])
    w_in = dram("w_in", [D, 24576])
    w_br = dram("w_br", [D, D])
    w_out = dram("w_out", [D, D])
    out = dram("out", [n_own * T, D], kind="ExternalOutput")
    c_ident = dram("c_ident", [128, 128])
    c_cmask = dram("c_cmask", [128, 128])
    c_gmask = dram("c_gmask", [128, 128])
    c_smask = dram("c_smask", [128, 512])
    c_lohi = dram("c_lohi", [128, 2])
    p_normw = dram("p_normw", [128, 32])
    p_lw = dram("p_lw", [128, 16])
    p_hgw = dram("p_hgw", [128, 16])
    p_l0 = dram("p_l0", [128, 16])
    p_l1 = dram("p_l1", [128, 16])
    p_fw = dram("p_fw", [128, D])
    p_ws = dram("p_ws", [128, 16 * 128])
    p_bs = dram("p_bs", [128, 16])
    p_lnb = dram("p_lnb", [128, 2048])

    P = Prog(nc)
    sb = nc.alloc_sbuf_tensor
    if dbg:
        d_hT = nc.dram_tensor("d_hT", [128, 32 * T], BF16, kind="ExternalOutput").ap()
        d_hA = nc.dram_tensor("d_hA", [128, 16 * T], BF16, kind="ExternalOutput").ap()
        d_hB = nc.dram_tensor("d_hB", [128, 16 * T], BF16, kind="ExternalOutput").ap()
        d_mg = nc.dram_tensor("d_mg", [128, 32 * T], BF16, kind="ExternalOutput").ap()

    arena = sb("arena", [128, NPG * 256], F32)
    pgb = [Buf("pg%d" % i) for i in range(NPG)]

    def A32(p0, n):
        return arena[:, p0 * 256:(p0 + n) * 256]

    def A16(p0, n):
        return arena[:, p0 * 256:(p0 + n) * 256].bitcast(BF16)

    def PB(p0, n):
        return pgb[p0:p0 + n]

    ident_f = sb("ident_f", [128, 128], F32); b_ident_f = Buf("ident_f")
    ident_b = sb("ident_b", [128, 128], BF16); b_ident_b = Buf("ident_b")
    cmask = sb("cmask", [128, 128], F32); b_cmask = Buf("cmask")
    smask = sb("smask", [128, 512], F32); b_smask = Buf("smask")
    lohi = sb("lohi", [128, 2], F32); b_lohi = Buf("lohi")
    normw = sb("normw", [128, 32], F32); b_normw = Buf("normw")
    lw = sb("lw", [128, 16], F32); b_lw = Buf("lw")
    hgw = sb("hgw", [128, 16], F32); b_hgw = Buf("hgw")
    l0 = sb("l0", [128, 16], F32); b_l0 = Buf("l0")
    l1 = sb("l1", [128, 16], F32); b_l1 = Buf("l1")
    lbt = sb("lbt", [128, 16], F32); b_lbt = Buf("lbt")
    oml = sb("oml", [128, 16], F32); b_oml = Buf("oml")
    ones_f = sb("ones_f", [128, 128], BF16); b_ones = Buf("ones")
    WmT = sb("WmT", [128, 16, 128], BF16); b_WmT = Buf("WmT")
    C2 = sb("C2", [128, 16, 128], F32); b_C2 = Buf("C2")
    S_f = sb("S_f", [128, 16, 128], F32); b_Sf = [Buf("Sf%d" % h) for h in range(16)]
    S_b = sb("S_b", [128, 16, 128], BF16); b_Sb = [Buf("Sb%d" % h) for h in range(16)]
    NSLOT = 3
    Wt = [sb("W%d" % i, [128, 32, GW], BF16) for i in range(NSLOT)]
    b_W = [Buf("W%d" % i) for i in range(NSLOT)]
    stat = sb("stat", [128, 64], F32)
    b_stat = [Buf("stat%d" % i) for i in range(64)]
    bnst = sb("bnst", [128, 4, 6], F32); b_bnst = Buf("bnst")
    mv = sb("mv", [128, 2], F32); b_mv = Buf("mv")

    NBANK = 8
    psum = [nc.alloc_psum_tensor("ps%d" % i, [128, 512], F32) for i in range(NBANK)]
    b_psh = [[Buf("ps%d_%d" % (i, hh)) for hh in range(2)] for i in range(NBANK)]
    b_ps = b_psh
    bank_ctr = [0]

    def bank():
        i = bank_ctr[0] % NBANK
        bank_ctr[0] += 1
        return i

    stat_ctr = [0]

    def stat_slot():
        i = stat_ctr[0] % 64
        stat_ctr[0] += 1
        return i

    wctr = [0]
    NGRP = 128
    wbf = nc.dram_tensor("wbf", [NGRP, 128, 32 * GW], BF16, kind="Internal").ap()
    b_wbf = [Buf("wbf%d" % i) for i in range(NGRP)]
    converted = set()

    def load_w(src_ap, gid, nk=32):
        s = wctr[0] % NSLOT
        wctr[0] += 1
        if gid in converted:
            src = wbf[gid, :, 0:nk * GW].rearrange("p (k c) -> p k c", c=GW)
            P.op("sp", lambda e, s=s, src=src, nk=nk: e.dma_start(out=Wt[s][:, 0:nk, :], in_=src),
                 reads=[b_wbf[gid]], writes=[b_W[s]], dma="w%d" % s)
        else:
            src = src_ap.rearrange("(k p) c -> p k c", p=128)
            P.op("pool", lambda e, s=s, src=src, nk=nk: e.dma_start(out=Wt[s][:, 0:nk, :], in_=src),
                 writes=[b_W[s]], dma="w%d" % s)
            if REUSE_BF16:
                dst = wbf[gid, :, 0:nk * GW].rearrange("p (k c) -> p k c", c=GW)
                P.op("sp", lambda e, s=s, dst=dst, nk=nk: e.dma_start(out=dst, in_=Wt[s][:, 0:nk, :]),
                     reads=[b_W[s]], writes=[b_wbf[gid]], dma="wb%d" % s)
                converted.add(gid)
        return s

    pcctr = [0]
    b_pc = [Buf("pc%d" % i) for i in range(4)]

    def preconvert(src_ap, gid, nk=32):
        if gid in converted:
            return
        i = pcctr[0] % 4
        pcctr[0] += 1
        src = src_ap.rearrange("(k p) c -> p k c", p=128)
        dst = wbf[gid, :, 0:nk * GW].rearrange("p (k c) -> p k c", c=GW)
        P.op("pool", lambda e, src=src, dst=dst: e.dma_start(out=dst, in_=src), writes=[b_wbf[gid], b_pc[i]], dma="pc%d" % i)
        converted.add(gid)

    def preconvert_plan(n):
        order = []
        for nm in ("u", "v", "za"):
            for g in range(8):
                order.append((nm, g))
        for hp in range(8):
            for nm in ("q", "og", "zb"):
                order.append((nm, hp))
        for nm, g in order[:n]:
            preconvert(w_in[:, COL[nm] + g * GW: COL[nm] + (g + 1) * GW], COL[nm] // GW + g)

    def preconvert_tail():
        for dg in range(16):
            preconvert(w_in[:, COL["ga"] + dg * GW: COL["ga"] + (dg + 1) * GW], COL["ga"] // GW + dg)
            preconvert(w_in[:, COL["gb"] + dg * GW: COL["gb"] + (dg + 1) * GW], COL["gb"] // GW + dg)
            preconvert(w_br[:, dg * GW:(dg + 1) * GW], 96 + dg)
        for og in range(16):
            preconvert(w_out[:, og * GW:(og + 1) * GW], 112 + og)

    def ld(dst, src, bufs, q="sp", key="ld_c"):
        P.op(q, lambda e: e.dma_start(out=dst, in_=src), writes=bufs, dma=key)

    ld(ident_f[:], c_ident, [b_ident_f], key="cc1")
    ld(cmask[:], c_cmask, [b_cmask], key="cc2")
    ld(lohi[:], c_lohi, [b_lohi], key="cc_lohi")
    ld(smask[:], c_smask, [b_smask], key="cc3")
    ld(normw[:], p_normw, [b_normw], key="cc4")
    ld(lw[:], p_lw, [b_lw], key="cc5")
    ld(hgw[:], p_hgw, [b_hgw], key="cc6")
    ld(l0[:], p_l0, [b_l0], key="cc7")
    ld(l1[:], p_l1, [b_l1], key="cc8")
    SC = 64
    ws_raw = A32(SC, 8)
    bs_col = A32(SC + 8, 1)[:, 0:16]
    lnb_bc = A32(SC + 16, 8)
    gmask = A32(SC + 24, 1)[:, 0:128]
    wm_tmp = A32(SC + 25, 1)[:, 0:128]
    wmT_f = A32(SC + 26, 1)[:, 0:128]
    cs_col = A32(SC + 27, 1)[:, 0:1]
    c2t = A32(SC + 28, 1)[:, 0:128]
    ld(ws_raw, p_ws, PB(SC, 8), key="cc9")
    ld(bs_col, p_bs, PB(SC + 8, 1), key="cc10")
    ld(lnb_bc, p_lnb, PB(SC + 16, 8), key="cc11")
    ld(gmask, c_gmask, PB(SC + 24, 1), key="cc12")

    P.op("dve", lambda e: e.memset(ones_f[:], 1.0), writes=[b_ones])
    P.op("dve", lambda e: e.memset(S_f[:], 0.0), writes=b_Sf)
    P.op("dve", lambda e: e.memset(S_b[:], 0.0), writes=b_Sb)
    P.op("dve", lambda e: e.tensor_copy(out=ident_b[:], in_=ident_f[:]), reads=[b_ident_f], writes=[b_ident_b])
    P.op("dve", lambda e: e.tensor_tensor(out=lbt[:], in0=l1[:], in1=l0[:], op=ALU.subtract), reads=[b_l0, b_l1], writes=[b_lbt])
    P.op("act", lambda e: e.activation(out=oml[:], in_=lbt[:], func=AF.Sigmoid), reads=[b_lbt], writes=[b_oml])

    for g in range(16):
        P.op("dve", lambda e, g=g: e.tensor_tensor(out=wm_tmp, in0=ws_raw[:, g * 128:(g + 1) * 128], in1=gmask, op=ALU.mult),
             reads=PB(SC, 8) + PB(SC + 24, 1), writes=PB(SC + 25, 1))
        bk = bank()
        P.op("pe", lambda e, bk=bk: e.transpose(out=psum[bk][:, 0:128], in_=wm_tmp, identity=ident_f[:]),
             reads=PB(SC + 25, 1) + [b_ident_f], writes=[b_ps[bk]])
        P.op("act", lambda e, bk=bk, g=g: e.activation(out=WmT[:, g, :], in_=psum[bk][:, 0:128], func=AF.Copy),
             reads=[b_ps[bk]], writes=[b_WmT])
        P.op("dve", lambda e: e.tensor_reduce(out=cs_col, in_=wm_tmp, axis=mybir.AxisListType.X, op=ALU.add),
             reads=PB(SC + 25, 1), writes=PB(SC + 27, 1))
        P.op("dve", lambda e, g=g: e.tensor_scalar(out=c2t, in0=lnb_bc[:, g * 128:(g + 1) * 128], scalar1=cs_col,
                                                    scalar2=bs_col[:, g:g + 1], op0=ALU.mult, op1=ALU.add),
             reads=PB(SC + 16, 8) + PB(SC + 27, 1) + PB(SC + 8, 1), writes=PB(SC + 28, 1))
        bk3 = bank()
        P.op("pe", lambda e, bk3=bk3: e.transpose(out=psum[bk3][:, 0:128], in_=c2t, identity=ident_f[:]),
             reads=PB(SC + 28, 1) + [b_ident_f], writes=[b_ps[bk3]])
        P.op("act", lambda e, bk3=bk3, g=g: e.activation(out=C2[:, g, :], in_=psum[bk3][:, 0:128], func=AF.Copy),
             reads=[b_ps[bk3]], writes=[b_C2])

    HT0, HA0, HB0 = 0, 32, 48
    hT = A16(HT0, 32).rearrange("p (k t) -> p k t", t=T)
    hA = A16(HA0, 16).rearrange("p (k t) -> p k t", t=T)
    hB = A16(HB0, 16).rearrange("p (k t) -> p k t", t=T)

    out_toks = []

    def phase0(src, r0):
        for blk in range(4):
            xp0 = SC + 16 * (blk % 2)
            xb = A32(xp0, 16)
            junk = A16(SC + 32, 8)
            P.op("sp", lambda e, xb=xb, r=r0 + blk * 128: e.dma_start(out=xb, in_=src[r:r + 128, :]),
                 writes=PB(xp0, 16), dma="x%d" % (blk % 2))
            s0 = stat_slot(); s1 = stat_slot(); s2 = stat_slot()
            P.op("act", lambda e, xb=xb, s0=s0: e.activation(out=junk, in_=xb, func=AF.Square, accum_out=stat[:, s0:s0 + 1]),
                 reads=PB(xp0, 16), writes=PB(SC + 32, 8) + [b_stat[s0]])
            P.op("dve", lambda e, s0=s0, s1=s1: e.tensor_scalar(out=stat[:, s1:s1 + 1], in0=stat[:, s0:s0 + 1], scalar1=1.0 / D,
                                                                 scalar2=EPS, op0=ALU.mult, op1=ALU.add),
                 reads=[b_stat[s0]], writes=[b_stat[s1]])
            P.op("act", lambda e, s1=s1, s2=s2: e.activation(out=stat[:, s2:s2 + 1], in_=stat[:, s1:s1 + 1], func=AF.Ln),
                 reads=[b_stat[s1]], writes=[b_stat[s2]])
            P.op("act", lambda e, s1=s1, s2=s2: e.activation(out=stat[:, s1:s1 + 1], in_=stat[:, s2:s2 + 1], func=AF.Exp, scale=-0.5),
                 reads=[b_stat[s2]], writes=[b_stat[s1]])
            P.op("dve", lambda e, xb=xb, s1=s1: e.tensor_scalar(out=xb, in0=xb, scalar1=stat[:, s1:s1 + 1], scalar2=None, op0=ALU.mult),
                 reads=PB(xp0, 16) + [b_stat[s1]], writes=PB(xp0, 16))
            for kb in range(8):
                bk = bank()

                def tr(e, bk=bk, kb=kb, xb=xb):
                    ins = None
                    for j in range(4):
                        kc = kb * 4 + j
                        ins = e.transpose(out=psum[bk][:, j * 128:(j + 1) * 128], in_=xb[:, kc * 128:(kc + 1) * 128],
                                          identity=ident_f[:])
                    return ins
                P.op("pe", tr, reads=PB(xp0, 16) + [b_ident_f], writes=[b_ps[bk]])
                for j in range(4):
                    kc = kb * 4 + j
                    if kb % 2 == 0:
                        P.op("act", lambda e, bk=bk, j=j, kc=kc, blk=blk: e.activation(
                            out=hT[:, kc, blk * 128:(blk + 1) * 128], in_=psum[bk][:, j * 128:(j + 1) * 128],
                            func=AF.Identity, scale=normw[:, kc:kc + 1]),
                            reads=[b_ps[bk], b_normw], writes=PB(HT0 + kc, 1))
                    else:
                        P.op("dve", lambda e, bk=bk, j=j, kc=kc, blk=blk: e.tensor_scalar(
                            out=hT[:, kc, blk * 128:(blk + 1) * 128], in0=psum[bk][:, j * 128:(j + 1) * 128],
                            scalar1=normw[:, kc:kc + 1], scalar2=None, op0=ALU.mult),
                            reads=[b_ps[bk], b_normw], writes=PB(HT0 + kc, 1))

    def proj_fm(s, nk, rhs_of, rhs_bufs, koff=0):
        bks = []
        for cb in range(2):
            bk = bank()
            bks.append(bk)

            def mm(e, bk=bk, cb=cb):
                ins = None
                for kc in range(nk):
                    ins = e.matmul(psum[bk][:, :], lhsT=Wt[s][:, koff + kc, cb * 128:(cb + 1) * 128], rhs=rhs_of(kc),
                                   start=(kc == 0), stop=(kc == nk - 1))
                return ins
            P.op("pe", mm, reads=[b_W[s]] + rhs_bufs, writes=[b_ps[bk]])
        return bks

    def proj_tm(s, lhs_of, lhs_bufs):
        res = []
        bk = None
        for blk in range(4):
            bk = bank()
            c0 = 0

            def mm(e, bk=bk, blk=blk, c0=c0):
                ins = None
                for kc in range(32):
                    ins = e.matmul(psum[bk][:, c0:c0 + GW], lhsT=lhs_of(kc, blk), rhs=Wt[s][:, kc, :],
                                   start=(kc == 0), stop=(kc == 31))
                return ins
            P.op("pe", mm, reads=[b_W[s]] + lhs_bufs, writes=[b_ps[bk]])
            res.append((bk, c0))
        return res

    hT_bufs = PB(HT0, 32)

    def hT_rhs(kc):
        return hT[:, kc, :]

    def hT_lhs(kc, blk):
        return hT[:, kc, blk * 128:(blk + 1) * 128]

    def phase1():
        for ug in range(8):
            s = load_w(w_in[:, COL["u"] + ug * GW: COL["u"] + (ug + 1) * GW], COL["u"] // GW + ug)
            bks = proj_fm(s, 32, hT_rhs, hT_bufs)
            for cb, bk in enumerate(bks):
                g = ug * 2 + cb
                P.op("act", lambda e, bk=bk, g=g: e.activation(out=hA[:, g, :], in_=psum[bk][:, :], func=AF.Gelu_apprx_tanh),
                     reads=[b_ps[bk]], writes=PB(HA0 + g, 1))
        gv = A16(SC, 16).rearrange("p (b c) -> p b c", c=2048)
        for vg in range(8):
            s = load_w(w_in[:, COL["v"] + vg * GW: COL["v"] + (vg + 1) * GW], COL["v"] // GW + vg)
            res = proj_tm(s, hT_lhs, hT_bufs)
            for blk, (bk, c0) in enumerate(res):
                P.op("act", lambda e, bk=bk, c0=c0, blk=blk, vg=vg: e.activation(
                    out=gv[:, blk, vg * GW:(vg + 1) * GW], in_=psum[bk][:, c0:c0 + GW], func=AF.Gelu_apprx_tanh),
                    reads=[b_ps[bk]], writes=PB(SC + blk * 4 + vg // 2, 1))
        for blk in range(4):
            for c in range(4):
                P.op("dve", lambda e, blk=blk, c=c: e.bn_stats(out=bnst[:, c, :], in_=gv[:, blk, c * 512:(c + 1) * 512]),
                     reads=PB(SC + blk * 4 + c, 1), writes=[b_bnst])
            P.op("dve", lambda e: e.bn_aggr(out=mv[:], in_=bnst[:].rearrange("p a b -> p (a b)")), reads=[b_bnst], writes=[b_mv])
            s1 = stat_slot(); s2 = stat_slot(); s3 = stat_slot()
            P.op("dve", lambda e, s1=s1: e.tensor_scalar(out=stat[:, s1:s1 + 1], in0=mv[:, 1:2], scalar1=EPS, scalar2=None, op0=ALU.add),
                 reads=[b_mv], writes=[b_stat[s1]])
            P.op("act", lambda e, s1=s1, s2=s2: e.activation(out=stat[:, s2:s2 + 1], in_=stat[:, s1:s1 + 1], func=AF.Ln),
                 reads=[b_stat[s1]], writes=[b_stat[s2]])
            P.op("act", lambda e, s1=s1, s2=s2: e.activation(out=stat[:, s1:s1 + 1], in_=stat[:, s2:s2 + 1], func=AF.Exp, scale=-0.5),
                 reads=[b_stat[s2]], writes=[b_stat[s1]])
            P.op("dve", lambda e, s1=s1, s3=s3: e.tensor_scalar(out=stat[:, s3:s3 + 1], in0=mv[:, 0:1], scalar1=stat[:, s1:s1 + 1],
                                                                 scalar2=-1.0, op0=ALU.mult, op1=ALU.mult),
                 reads=[b_mv, b_stat[s1]], writes=[b_stat[s3]])
            P.op("dve", lambda e, blk=blk, s1=s1, s3=s3: e.tensor_scalar(
                out=gv[:, blk, :], in0=gv[:, blk, :], scalar1=stat[:, s1:s1 + 1], scalar2=stat[:, s3:s3 + 1],
                op0=ALU.mult, op1=ALU.add),
                reads=PB(SC + blk * 4, 4) + [b_stat[s1], b_stat[s3]], writes=PB(SC + blk * 4, 4))
        for g in range(16):
            bk = bank()

            def mm(e, bk=bk, g=g):
                ins = None
                for blk in range(4):
                    ins = e.matmul(psum[bk][:, blk * 128:(blk + 1) * 128], lhsT=gv[:, blk, g * 128:(g + 1) * 128],
                                   rhs=WmT[:, g, :], start=True, stop=True)
                return ins
            P.op("pe", mm, reads=PB(SC, 16) + [b_WmT], writes=[b_ps[bk]])
            tp = SC + 16 + 2 * (g % 2)
            t1 = A32(tp, 2)
            P.op("dve", lambda e, bk=bk, g=g, t1=t1: e.scalar_tensor_tensor(
                out=t1.rearrange("p (b i) -> p b i", i=128), in0=psum[bk][:, :].rearrange("p (b i) -> p b i", i=128),
                scalar=lw[:, g:g + 1], in1=C2[:, g:g + 1, :].broadcast_to([128, 4, 128]), op0=ALU.mult, op1=ALU.add),
                reads=[b_ps[bk], b_lw, b_C2], writes=PB(tp, 2))
            P.op("dve", lambda e, g=g, t1=t1: e.tensor_tensor(out=hA[:, g, :], in0=hA[:, g, :], in1=t1, op=ALU.mult),
                 reads=PB(tp, 2) + PB(HA0 + g, 1), writes=PB(HA0 + g, 1))
        for zg in range(8):
            s = load_w(w_in[:, COL["za"] + zg * GW: COL["za"] + (zg + 1) * GW], COL["za"] // GW + zg)
            bks = proj_fm(s, 32, hT_rhs, hT_bufs)
            for cb, bk in enumerate(bks):
                g = zg * 2 + cb
                tp = SC + 20 + 2 * (g % 2)
                t1 = A32(tp, 2)
                P.op("act", lambda e, bk=bk, t1=t1: e.activation(out=t1, in_=psum[bk][:, :], func=AF.Silu),
                     reads=[b_ps[bk]], writes=PB(tp, 2))
                P.op("dve", lambda e, g=g, t1=t1: e.tensor_tensor(out=hA[:, g, :], in0=hA[:, g, :], in1=t1, op=ALU.mult),
                     reads=PB(tp, 2) + PB(HA0 + g, 1), writes=PB(HA0 + g, 1))

    INP0 = SC
    inpT = A16(INP0, 16).rearrange("p (b c) -> p b c", c=2048)

    def head_temps(hl):
        base = SC + 16 + hl * 24
        d = {}
        names = [("tk", 2), ("tq", 2), ("tf", 2), ("tc", 2), ("kp", 1), ("kd", 2), ("kdT", 2), ("qd", 1), ("sT", 1),
                 ("ver", 2), ("sog", 1), ("szb", 1), ("oc", 2), ("osq", 2)]
        p = base
        for n, k in names:
            d[n] = (p, k)
            p += k
        assert p <= base + 24
        d["Sv"] = (base, 4)
        return d

    class HC:
        def __init__(self, hl):
            self.tm = head_temps(hl)

        def v32(self, n):
            return A32(*self.tm[n])

        def v16(self, n):
            return A16(*self.tm[n])

        def b(self, n):
            return PB(*self.tm[n])

    def evac_f(hl, bk):
        c = HC(hl)
        P.op("act", lambda e, bk=bk, tk=c.v32("tk"): e.activation(out=tk, in_=psum[bk][:, :], func=AF.Sigmoid, scale=-1.0),
             reads=[b_ps[bk]], writes=c.b("tk"))

    def fchain(h, hl):
        c = HC(hl)
        tk, tf, tc = c.v32("tk"), c.v32("tf"), c.v32("tc")
        kp, kd = c.v16("kp"), c.v32("kd")
        P.op("dve", lambda e: e.tensor_scalar(out=tk, in0=tk, scalar1=oml[:, h:h + 1], scalar2=None, op0=ALU.mult),
             reads=c.b("tk") + [b_oml], writes=c.b("tk"))
        P.op("dve", lambda e: e.tensor_scalar(out=tf, in0=tk, scalar1=-1.0, scalar2=1.0, op0=ALU.mult, op1=ALU.add),
             reads=c.b("tk"), writes=c.b("tf"))
        P.op("act", lambda e: e.activation(out=tf, in_=tf, func=AF.Ln), reads=c.b("tf"), writes=c.b("tf"))
        P.op("dve", lambda e: e.tensor_tensor_scan(out=tc, data0=smask[:], data1=tf, initial=0.0, op0=ALU.mult, op1=ALU.add),
             reads=c.b("tf") + [b_smask], writes=c.b("tc"))
        P.op("act", lambda e: e.activation(out=tf, in_=tc, func=AF.Exp), reads=c.b("tc"), writes=c.b("tf"))
        P.op("act", lambda e: e.activation(out=tc, in_=tc, func=AF.Exp, scale=-1.0), reads=c.b("tc"), writes=c.b("tc"))
        P.op("dve", lambda e: e.tensor_tensor(out=kp, in0=tk, in1=tc, op=ALU.mult),
             reads=c.b("tk") + c.b("tc"), writes=c.b("kp"))
        tA3 = tf.rearrange("p (n i) -> p n i", i=64)
        P.op("dve", lambda e: e.tensor_tensor(
            out=kd.rearrange("p (n i) -> p n i", i=64), in0=kp.rearrange("p (n i) -> p n i", i=64),
            in1=tA3[:, :, 63:64].broadcast_to([128, 8, 64]), op=ALU.mult),
            reads=c.b("kp") + c.b("tf"), writes=c.b("kd"))

    def kdt_stage(h, hl):
        c = HC(hl)
        kd, kdT = c.v32("kd"), c.v16("kdT")
        bkT = bank()

        def trk(e):
            ins = None
            for blk in range(4):
                ins = e.transpose(out=psum[bkT][:, blk * 128:(blk + 1) * 128], in_=kd[:, blk * 128:(blk + 1) * 128],
                                  identity=ident_f[:])
            return ins
        P.op("pe", trk, reads=c.b("kd") + [b_ident_f], writes=[b_ps[bkT]])
        P.op("act", lambda e: e.activation(out=kdT[:, 0:512], in_=psum[bkT][:, :], func=AF.Identity, scale=lohi[:, 0:1]),
             reads=[b_ps[bkT], b_lohi], writes=c.b("kdT"))
        P.op("dve", lambda e: e.tensor_scalar(out=kdT[:, 512:1024], in0=psum[bkT][:, :], scalar1=lohi[:, 1:2], scalar2=None,
                                              op0=ALU.mult),
             reads=[b_ps[bkT], b_lohi], writes=c.b("kdT"))

    def state_stage(h, hl, own):
        c = HC(hl)
        kdT, tf = c.v16("kdT"), c.v32("tf")
        bkU = [bank(), bank()]

        def umm(e):
            ins = None
            for n in range(8):
                blk, ch = n // 2, n % 2
                ins = e.matmul(psum[bkU[n // 4]][:, (n % 4) * 128:(n % 4 + 1) * 128],
                               lhsT=kdT[:, ch * 512 + blk * 128:ch * 512 + (blk + 1) * 128],
                               rhs=inpT[:, blk, h * 128:(h + 1) * 128], start=True, stop=True)
            return ins
        P.op("pe", umm, reads=c.b("kdT") + PB(INP0, 16), writes=[b_ps[bkU[0]], b_ps[bkU[1]]])
        Sv = A32(*c.tm["Sv"]).rearrange("p (n v) -> p n v", v=128)
        ver = c.v16("ver").rearrange("p (n v) -> p n v", v=128)
        if own:
            P.op("act", lambda e: e.activation(out=ver[:, 0, :], in_=S_b[:, h, :], func=AF.Copy),
                 reads=[b_Sb[h]], writes=c.b("ver"))
        for n in range(8):
            prev = S_f[:, h, :] if n == 0 else Sv[:, n - 1, :]
            P.op("dve", lambda e, n=n, prev=prev: e.scalar_tensor_tensor(
                out=Sv[:, n, :], in0=prev, scalar=tf[:, n * 64 + 63:n * 64 + 64],
                in1=psum[bkU[n // 4]][:, (n % 4) * 128:(n % 4 + 1) * 128], op0=ALU.mult, op1=ALU.add),
                reads=[b_Sf[h], b_ps[bkU[n // 4]]] + c.b("tf") + c.b("Sv"), writes=c.b("Sv"))
        if own:
            P.op("act", lambda e: e.activation(out=ver[:, 1:8, :], in_=Sv[:, 0:7, :], func=AF.Copy),
                 reads=c.b("Sv"), writes=c.b("ver"))
        P.op("act", lambda e: e.activation(out=S_b[:, h, :], in_=Sv[:, 7, :], func=AF.Copy), reads=c.b("Sv"), writes=[b_Sb[h]])
        P.op("dve", lambda e: e.tensor_copy(out=S_f[:, h, :], in_=Sv[:, 7, :]), reads=c.b("Sv"), writes=[b_Sf[h]])

    def score_stage(h, hl):
        c = HC(hl)
        kp, qd, sT = c.v16("kp"), c.v16("qd"), c.v16("sT")
        bkS = bank()

        def smm(e):
            ins = None
            for blk in range(4):
                ins = e.matmul(psum[bkS][:, blk * 128:(blk + 1) * 128], lhsT=kp[:, blk * 128:(blk + 1) * 128],
                               rhs=qd[:, blk * 128:(blk + 1) * 128], start=True, stop=True)
            return ins
        P.op("pe", smm, reads=c.b("kp") + c.b("qd"), writes=[b_ps[bkS]])
        P.op("dve", lambda e: e.tensor_tensor(
            out=sT.rearrange("p (b i) -> p b i", i=128), in0=psum[bkS][:, :].rearrange("p (b i) -> p b i", i=128),
            in1=cmask[:].rearrange("p (o i) -> p o i", o=1).broadcast_to([128, 4, 128]), op=ALU.mult),
            reads=[b_ps[bkS], b_cmask], writes=c.b("sT"))

    def out_stage(h, hl):
        c = HC(hl)
        qd, sT = c.v16("qd"), c.v16("sT")
        ver = c.v16("ver").rearrange("p (n v) -> p n v", v=128)
        bkO = bank()
        bkI = bank()

        def omm(e):
            ins = None
            for blk in range(4):
                e.matmul(psum[bkO][:, blk * 128:(blk + 1) * 128], lhsT=inpT[:, blk, h * 128:(h + 1) * 128],
                         rhs=sT[:, blk * 128:(blk + 1) * 128], start=True, stop=True)
            for n in range(8):
                c0 = n * 64
                ins = e.matmul(psum[bkI][:, c0:c0 + 64], lhsT=ver[:, n, :], rhs=qd[:, c0:c0 + 64], start=True, stop=True)
            return ins
        P.op("pe", omm, reads=c.b("sT") + c.b("qd") + c.b("ver") + PB(INP0, 16), writes=[b_ps[bkO], b_ps[bkI]])
        oc, osq = c.v32("oc"), c.v32("osq")
        osqb = c.v16("osq")[:, 0:512]
        P.op("act", lambda e: e.activation(out=oc, in_=psum[bkI][:, :], func=AF.Copy), reads=[b_ps[bkI]], writes=c.b("oc"))
        P.op("dve", lambda e: e.tensor_tensor(out=oc, in0=psum[bkO][:, :], in1=oc, op=ALU.add),
             reads=[b_ps[bkO]] + c.b("oc"), writes=c.b("oc"))
        P.op("act", lambda e: e.activation(out=osqb, in_=oc, func=AF.Square), reads=c.b("oc"), writes=c.b("osq"))

    def norm_stage(h, hl):
        c = HC(hl)
        oc, osq = c.v32("oc"), c.v32("osq")
        osqb = c.v16("osq")[:, 0:512]
        sog, szb = c.v16("sog"), c.v16("szb")
        bkN = bank()
        P.op("pe", lambda e: e.matmul(psum[bkN][:, :], lhsT=ones_f[:], rhs=osqb, start=True, stop=True),
             reads=c.b("osq") + [b_ones], writes=[b_ps[bkN]])
        P.op("dve", lambda e: e.tensor_scalar(out=osq, in0=psum[bkN][:, :], scalar1=1.0 / 128, scalar2=EPS, op0=ALU.mult, op1=ALU.add),
             reads=[b_ps[bkN]], writes=c.b("osq"))
        P.op("act", lambda e: e.activation(out=osq, in_=osq, func=AF.Ln), reads=c.b("osq"), writes=c.b("osq"))
        P.op("act", lambda e: e.activation(out=osq, in_=osq, func=AF.Exp, scale=-0.5), reads=c.b("osq"), writes=c.b("osq"))
        P.op("dve", lambda e: e.tensor_tensor(out=oc, in0=oc, in1=osq, op=ALU.mult), reads=c.b("oc") + c.b("osq"), writes=c.b("oc"))
        P.op("dve", lambda e: e.tensor_tensor(out=oc, in0=oc, in1=sog, op=ALU.mult), reads=c.b("oc") + c.b("sog"), writes=c.b("oc"))
        P.op("dve", lambda e: e.scalar_tensor_tensor(out=hB[:, h, :], in0=oc, scalar=hgw[:, h:h + 1], in1=szb,
                                                     op0=ALU.mult, op1=ALU.mult),
             reads=c.b("oc") + c.b("szb") + [b_hgw], writes=PB(HB0 + h, 1))

    def wproj(name, hp):
        s = load_w(w_in[:, COL[name] + hp * GW: COL[name] + (hp + 1) * GW], COL[name] // GW + hp)
        return proj_fm(s, 32, hT_rhs, hT_bufs)

    def phase2(own):
        for ig in range(8):
            s = load_w(w_in[:, COL["inp"] + ig * GW: COL["inp"] + (ig + 1) * GW], COL["inp"] // GW + ig)
            res = proj_tm(s, hT_lhs, hT_bufs)
            for blk, (bk, c0) in enumerate(res):
                P.op("act", lambda e, bk=bk, c0=c0, blk=blk, ig=ig: e.activation(
                    out=inpT[:, blk, ig * GW:(ig + 1) * GW], in_=psum[bk][:, c0:c0 + GW], func=AF.Copy),
                    reads=[b_ps[bk]], writes=PB(INP0 + blk * 4 + ig // 2, 1))
        if not own:
            bkF = wproj("f", 0)
            for hl in range(2):
                evac_f(hl, bkF[hl])
            for hl in range(2):
                fchain(hl, hl)
            for hp in range(8):
                if hp + 1 < 8:
                    bkF = wproj("f", hp + 1)
                for hl in range(2):
                    kdt_stage(hp * 2 + hl, hl)
                for hl in range(2):
                    state_stage(hp * 2 + hl, hl, False)
                if hp + 1 < 8:
                    for hl in range(2):
                        evac_f(hl, bkF[hl])
                    for hl in range(2):
                        fchain((hp + 1) * 2 + hl, hl)
            return
        for hp in range(8):
            bkF = wproj("f", hp)
            for hl in range(2):
                evac_f(hl, bkF[hl])
            for hl in range(2):
                fchain(hp * 2 + hl, hl)
            bkQ = wproj("q", hp)
            for hl in range(2):
                c = HC(hl)
                P.op("act", lambda e, bq=bkQ[hl], tq=c.v32("tq"): e.activation(out=tq, in_=psum[bq][:, :], func=AF.Silu),
                     reads=[b_ps[bkQ[hl]]], writes=c.b("tq"))
            for hl in range(2):
                c = HC(hl)
                P.op("dve", lambda e, tq=c.v32("tq"), tf=c.v32("tf"), qd=c.v16("qd"): e.tensor_tensor(out=qd, in0=tq, in1=tf, op=ALU.mult),
                     reads=c.b("tq") + c.b("tf"), writes=c.b("qd"))
            bkO2 = wproj("og", hp)
            for hl in range(2):
                c = HC(hl)
                P.op("act", lambda e, b=bkO2[hl], t=c.v16("sog"): e.activation(out=t, in_=psum[b][:, :], func=AF.Sigmoid),
                     reads=[b_ps[bkO2[hl]]], writes=c.b("sog"))
            bkZ = wproj("zb", hp)
            for hl in range(2):
                c = HC(hl)
                P.op("act", lambda e, b=bkZ[hl], t=c.v16("szb"): e.activation(out=t, in_=psum[b][:, :], func=AF.Silu),
                     reads=[b_ps[bkZ[hl]]], writes=c.b("szb"))
            for hl in range(2):
                kdt_stage(hp * 2 + hl, hl)
            for hl in range(2):
                score_stage(hp * 2 + hl, hl)
            for hl in range(2):
                state_stage(hp * 2 + hl, hl, True)
            for hl in range(2):
                out_stage(hp * 2 + hl, hl)
            for hl in range(2):
                norm_stage(hp * 2 + hl, hl)

    MG0 = SC
    merged = A16(MG0, 32).rearrange("p (k t) -> p k t", t=T)

    def phase3():
        for dg in range(16):
            sA = load_w(w_in[:, COL["ga"] + dg * GW: COL["ga"] + (dg + 1) * GW], COL["ga"] // GW + dg)
            bkGA = proj_fm(sA, 32, hT_rhs, hT_bufs)
            sB = load_w(w_in[:, COL["gb"] + dg * GW: COL["gb"] + (dg + 1) * GW], COL["gb"] // GW + dg)
            bkGB = proj_fm(sB, 32, hT_rhs, hT_bufs)
            sW = load_w(w_br[:, dg * GW:(dg + 1) * GW], 96 + dg)
            bkPA = proj_fm(sW, 16, lambda kc: hA[:, kc, :], PB(HA0, 16), koff=0)
            bkPB = proj_fm(sW, 16, lambda kc: hB[:, kc, :], PB(HB0, 16), koff=16)
            for cb in range(2):
                dmb = dg * 2 + cb
                tp = SC + 32 + 4 * cb
                ta, tb = A32(tp, 2), A32(tp + 2, 2)
                P.op("act", lambda e, b=bkGA[cb], ta=ta: e.activation(out=ta, in_=psum[b][:, :], func=AF.Sigmoid),
                     reads=[b_ps[bkGA[cb]]], writes=PB(tp, 2))
                P.op("act", lambda e, b=bkGB[cb], tb=tb: e.activation(out=tb, in_=psum[b][:, :], func=AF.Sigmoid),
                     reads=[b_ps[bkGB[cb]]], writes=PB(tp + 2, 2))
                P.op("dve", lambda e, b=bkPA[cb], ta=ta: e.tensor_tensor(out=ta, in0=psum[b][:, :], in1=ta, op=ALU.mult),
                     reads=[b_ps[bkPA[cb]]] + PB(tp, 2), writes=PB(tp, 2))
                P.op("dve", lambda e, b=bkPB[cb], tb=tb: e.tensor_tensor(out=tb, in0=psum[b][:, :], in1=tb, op=ALU.mult),
                     reads=[b_ps[bkPB[cb]]] + PB(tp + 2, 2), writes=PB(tp + 2, 2))
                P.op("dve", lambda e, ta=ta, tb=tb, dmb=dmb: e.tensor_tensor(out=merged[:, dmb, :], in0=ta, in1=tb, op=ALU.add),
                     reads=PB(tp, 4), writes=PB(MG0 + dmb, 1))

    Y0 = 0
    yv = A32(Y0, 64).rearrange("p (b c) -> p b c", c=D)
    FW0 = SC + 32
    fwb = A32(FW0, 16)

    def phase4(r0):
        for blk in range(4):
            P.op("sp", lambda e, blk=blk: e.dma_start(out=yv[:, blk, :], in_=x[r0 + blk * 128:r0 + (blk + 1) * 128, :]),
                 writes=PB(Y0 + blk * 16, 16), dma="y%d" % blk)
        P.op("sp", lambda e: e.dma_start(out=fwb, in_=p_fw), writes=PB(FW0, 16), dma="fw")
        for og in range(16):
            s = load_w(w_out[:, og * GW:(og + 1) * GW], 112 + og)
            res = proj_tm(s, lambda kc, blk: merged[:, kc, blk * 128:(blk + 1) * 128], PB(MG0, 32))
            for blk, (bk, c0) in enumerate(res):
                P.op("dve", lambda e, bk=bk, c0=c0, blk=blk, og=og: e.tensor_tensor(
                    out=yv[:, blk, og * GW:(og + 1) * GW], in0=psum[bk][:, c0:c0 + GW], in1=yv[:, blk, og * GW:(og + 1) * GW], op=ALU.add),
                    reads=[b_ps[bk]] + PB(Y0 + blk * 16 + og, 1), writes=PB(Y0 + blk * 16 + og, 1))
        junk = A16(MG0, 8)
        for blk in range(4):
            s0 = stat_slot(); s1 = stat_slot(); s2 = stat_slot()
            P.op("act", lambda e, blk=blk, s0=s0: e.activation(out=junk, in_=yv[:, blk, :], func=AF.Square, accum_out=stat[:, s0:s0 + 1]),
                 reads=PB(Y0 + blk * 16, 16), writes=PB(MG0, 8) + [b_stat[s0]])
            P.op("dve", lambda e, s0=s0, s1=s1: e.tensor_scalar(out=stat[:, s1:s1 + 1], in0=stat[:, s0:s0 + 1], scalar1=1.0 / D,
                                                                 scalar2=EPS, op0=ALU.mult, op1=ALU.add),
                 reads=[b_stat[s0]], writes=[b_stat[s1]])
            P.op("act", lambda e, s1=s1, s2=s2: e.activation(out=stat[:, s2:s2 + 1], in_=stat[:, s1:s1 + 1], func=AF.Ln),
                 reads=[b_stat[s1]], writes=[b_stat[s2]])
            P.op("act", lambda e, s1=s1, s2=s2: e.activation(out=stat[:, s1:s1 + 1], in_=stat[:, s2:s2 + 1], func=AF.Exp, scale=-0.5),
                 reads=[b_stat[s2]], writes=[b_stat[s1]])
            P.op("dve", lambda e, blk=blk, s1=s1: e.scalar_tensor_tensor(
                out=yv[:, blk, :], in0=yv[:, blk, :], scalar=stat[:, s1:s1 + 1], in1=fwb, op0=ALU.mult, op1=ALU.mult),
                reads=PB(Y0 + blk * 16, 16) + PB(FW0, 16) + [b_stat[s1]], writes=PB(Y0 + blk * 16, 16))
            tok = P.op("sp", lambda e, blk=blk: e.dma_start(out=out[r0 + blk * 128:r0 + (blk + 1) * 128, :], in_=yv[:, blk, :]),
                       reads=PB(Y0 + blk * 16, 16), dma="o%d" % blk)
            out_toks.append(tok)

    import os as _os
    stages = _os.environ.get("K_STAGES", "w0,w2,p0,p1,p2,p3,p4").split(",")
    for t in range(n_warm):
        if "w0" in stages:
            phase0(xp, t * T)
        if "w2" in stages:
            phase2(False)
        if t == 0 and n_warm > 1 and REUSE_BF16 and N_PRECONV > 0:
            preconvert_plan(N_PRECONV)
    for t in range(n_own):
        if t == 0 and n_warm > 1 and REUSE_BF16 and N_PRECONV >= 48:
            preconvert_tail()
        if "p0" in stages:
            phase0(x, t * T)
        if "p1" in stages:
            phase1()
        if "p2" in stages:
            phase2(True)
        if dbg and t == n_own - 1:
            out_toks.append(P.op("sp", lambda e: e.dma_start(out=d_hT, in_=A16(HT0, 32)), reads=PB(HT0, 32), dma="dbg0"))
            out_toks.append(P.op("sp", lambda e: e.dma_start(out=d_hA, in_=A16(HA0, 16)), reads=PB(HA0, 16), dma="dbg1"))
            out_toks.append(P.op("sp", lambda e: e.dma_start(out=d_hB, in_=A16(HB0, 16)), reads=PB(HB0, 16), dma="dbg2"))
        if "p3" in stages:
            phase3()
        if dbg and t == n_own - 1:
            out_toks.append(P.op("sp", lambda e: e.dma_start(out=d_mg, in_=A16(MG0, 32)), reads=PB(MG0, 32), dma="dbg3"))
        if "p4" in stages:
            phase4(t * T)
    P.wait_all("sp", out_toks)
    P.emit()
    return nc


def _consts():
    ident = np.eye(128, dtype=np.float32)
    idx = np.arange(128)
    ch = idx // 64
    cmask = ((ch[:, None] == ch[None, :]) & (idx[:, None] <= idx[None, :])).astype(np.float32)
    gmask = (ch[None, :] <= ch[:, None]).astype(np.float32)
    smask = np.ones((128, 512), np.float32)
    smask[:, ::64] = 0.0
    lohi = np.zeros((128, 2), np.float32)
    lohi[:64, 0] = 1.0
    lohi[64:, 1] = 1.0
    return ident, cmask, gmask, smask, lohi


def _param_maps(norm_w, gmlp_ln_w, gmlp_ln_b, gmlp_w_s, gmlp_b_s, hgrn_lb_logits, hgrn_norm_w, final_norm_w):
    ident, cmask, gmask, smask, lohi = _consts()
    f = np.float32
    return {
        "c_ident": ident, "c_cmask": cmask, "c_gmask": gmask, "c_smask": smask, "c_lohi": lohi,
        "p_normw": np.ascontiguousarray(norm_w[0].reshape(32, 128).T, dtype=f),
        "p_lw": np.ascontiguousarray(gmlp_ln_w[0].reshape(16, 128).T, dtype=f),
        "p_hgw": np.ascontiguousarray(hgrn_norm_w[0].reshape(16, 128).T, dtype=f),
        "p_l0": np.ascontiguousarray(hgrn_lb_logits[0].reshape(16, 128).T, dtype=f),
        "p_l1": np.ascontiguousarray(hgrn_lb_logits[1].reshape(16, 128).T, dtype=f),
        "p_fw": np.ascontiguousarray(np.broadcast_to(final_norm_w[None, :], (128, D)), dtype=f),
        "p_ws": np.ascontiguousarray(np.transpose(gmlp_w_s[0], (1, 0, 2)).reshape(128, 2048), dtype=f),
        "p_bs": np.ascontiguousarray(gmlp_b_s[0].T, dtype=f),
        "p_lnb": np.ascontiguousarray(np.broadcast_to(gmlp_ln_b[0][None, :], (128, 2048)), dtype=f),
    }


_NC_CACHE = {}


def kernel(x, norm_w, w_in, gmlp_ln_w, gmlp_ln_b, gmlp_w_s, gmlp_b_s, hgrn_lb_logits, hgrn_norm_w,
           w_branch, w_out, final_norm_w):
    x = np.asarray(x, dtype=np.float32)
    Bsz, S, _ = x.shape
    half = S // 2
    n_own = half // T
    n_warm = half // T
    key = (n_own, n_warm)
    if key not in _NC_CACHE:
        _NC_CACHE[key] = build(n_own, n_warm)
    nc = _NC_CACHE[key]
    pm = _param_maps(np.asarray(norm_w), np.asarray(gmlp_ln_w), np.asarray(gmlp_ln_b), np.asarray(gmlp_w_s),
                     np.asarray(gmlp_b_s), np.asarray(hgrn_lb_logits), np.asarray(hgrn_norm_w), np.asarray(final_norm_w))
    w_in0 = np.ascontiguousarray(np.asarray(w_in)[0], dtype=np.float32)
    w_br0 = np.ascontiguousarray(np.asarray(w_branch)[0].reshape(2 * 2048, D), dtype=np.float32)
    w_out0 = np.ascontiguousarray(np.asarray(w_out)[0], dtype=np.float32)
    zeros = np.zeros((half, D), np.float32)
    in_maps = []
    for c in range(8):
        b, hf = c // 2, c % 2
        m = dict(pm)
        m["x"] = np.ascontiguousarray(x[b, hf * half:(hf + 1) * half])
        m["xp"] = zeros if hf == 0 else np.ascontiguousarray(x[b, 0:half])
        m["w_in"] = w_in0
        m["w_br"] = w_br0
        m["w_out"] = w_out0
        in_maps.append(m)
    res = run_bass_kernel_spmd(nc, in_maps, core_ids=list(range(8)))
    outp = np.empty((Bsz, S, D), np.float32)
    for c in range(8):
        b, hf = c // 2, c % 2
        outp[b, hf * half:(hf + 1) * half] = res.results[c]["out"]
    return outp
```
